# Optimizing a Trainium2 kernel written in Bass

```python
import jax
import jax.numpy as jnp
from jax import lax
import numpy as np

D_MODEL = 2048
BATCH = 4
SEQ = 8192
DEPTH = 1

CTX_LEN = 256
GRID_W = 64
D_CONV = D_MODEL
CONV_WIDTH = 3
RET_HEADS = 8
RET_DK = D_MODEL // RET_HEADS
RET_DV = 2 * D_MODEL // RET_HEADS
RET_CHUNK = 128
ROPE_BASE = 10000.0
D_FF = 4 * D_MODEL
N_MOD = 6
EPS = 1e-6
IN_NAMES = ('conv_b', 'conv_c', 'conv_x', 'q', 'k', 'v', 'g', 'gate_conv', 'gate_ret')
IN_WIDTHS = (D_CONV, D_CONV, D_CONV, RET_HEADS * RET_DK, RET_HEADS * RET_DK,
             RET_HEADS * RET_DV, RET_HEADS * RET_DV, D_MODEL, D_MODEL)
IN_OFFSETS = tuple(int(o) for o in np.cumsum((0,) + IN_WIDTHS))
D_IN = IN_OFFSETS[-1]

kernel_name = 'hybrid_shortconv_retention_flow_block'


def _rmsnorm(x, gain):
    xf = x.astype(jnp.float32)
    y = xf * lax.rsqrt(jnp.mean(xf * xf, axis=-1, keepdims=True) + EPS)
    return (y * gain.astype(jnp.float32)).astype(x.dtype)


def _modulate(x, gain, shift, scale):
    return _rmsnorm(x, gain) * (1 + scale[:, None, :]) + shift[:, None, :]


def _combined_projection(h, w_in, names):
    out = {}
    for i, name in enumerate(IN_NAMES):
        if name in names:
            out[name] = h @ w_in[:, IN_OFFSETS[i]:IN_OFFSETS[i + 1]]
    return out


def _flip(t):
    return jnp.flip(t, axis=1)


def _centred_conv(u, conv_w):
    length = u.shape[-2]
    half = CONV_WIDTH // 2
    pad = [(0, 0)] * (u.ndim - 2) + [(half, half), (0, 0)]
    up = jnp.pad(u, pad)
    return sum(conv_w[i] * lax.slice_in_dim(up, i, i + length, axis=u.ndim - 2)
               for i in range(CONV_WIDTH))


def _short_conv_branch(z, conv_w, w_conv_out, rows):
    u = z['conv_c'] * z['conv_x']
    if rows is None:
        y = _centred_conv(u, conv_w)
    else:
        b, l, ch = u.shape
        y = _centred_conv(u.reshape(b, rows, GRID_W, ch), conv_w).reshape(b, l, ch)
    return (z['conv_b'] * y) @ w_conv_out


def _rotary(t, pos):
    half = RET_DK // 2
    inv_freq = 1.0 / (ROPE_BASE ** jnp.linspace(0.0, 1.0, half, dtype=jnp.float32))
    ang = pos[:, None] * inv_freq[None, :]
    cos = jnp.cos(ang)[None, :, None, :].astype(t.dtype)
    sin = jnp.sin(ang)[None, :, None, :].astype(t.dtype)
    t1, t2 = t[..., :half], t[..., half:]
    return jnp.concatenate([t1 * cos - t2 * sin, t1 * sin + t2 * cos], axis=-1)


def _heads(t, dim):
    b, l, _ = t.shape
    return t.reshape(b, l, RET_HEADS, dim)


def _retention_scan(q, k, v, log_gamma, state0):
    b, l, h, _ = q.shape
    n_chunks = l // RET_CHUNK
    dt = q.dtype
    idx = jnp.arange(RET_CHUNK, dtype=jnp.float32)
    rel = idx[:, None] - idx[None, :]
    intra = jnp.where(rel[None] >= 0,
                      jnp.exp(log_gamma[:, None, None] * jnp.maximum(rel, 0.0)[None]),
                      0.0).astype(dt)
    q_decay = jnp.exp(log_gamma[:, None] * (idx[None, :] + 1.0)).astype(dt)
    k_decay = jnp.exp(log_gamma[:, None] * (RET_CHUNK - 1.0 - idx[None, :])).astype(dt)
    chunk_decay = jnp.exp(log_gamma * RET_CHUNK).astype(dt)[None, :, None, None]

    def to_chunks(t):
        return jnp.moveaxis(t.reshape(b, n_chunks, RET_CHUNK, h, t.shape[-1]), 1, 0)

    def step(state, qkv):
        qc, kc, vc = qkv
        scores = jnp.einsum('bihd,bjhd->bhij', qc, kc) * intra
        o = jnp.einsum('bhij,bjhe->bihe', scores, vc)
        o = o + jnp.einsum('bihd,hi,bhde->bihe', qc, q_decay, state)
        state = chunk_decay * state + jnp.einsum('bjhd,hj,bjhe->bhde', kc, k_decay, vc)
        return state, o

    state, o = lax.scan(step, state0.astype(dt), (to_chunks(q), to_chunks(k), to_chunks(v)))
    return jnp.moveaxis(o, 0, 1).reshape(b, l, h, v.shape[-1]), state


def _final_state(k, v, log_gamma):
    l = k.shape[1]
    w = jnp.exp((l - 1.0 - jnp.arange(l, dtype=jnp.float32))[:, None]
                * log_gamma[None, :]).astype(k.dtype)
    return jnp.einsum('bjhd,jh,bjhe->bhde', k, w, v)


def _retention_out(o, g, w_ret_out):
    of = o.astype(jnp.float32)
    mu = jnp.mean(of, axis=-1, keepdims=True)
    var = jnp.mean(jnp.square(of - mu), axis=-1, keepdims=True)
    on = ((of - mu) * lax.rsqrt(var + EPS)).astype(g.dtype)
    b, l = o.shape[:2]
    return (jax.nn.silu(g) * on.reshape(b, l, RET_HEADS * RET_DV)) @ w_ret_out


def _merge(z, y_conv, y_ret, w_o):
    return (jax.nn.sigmoid(z['gate_conv']) * y_conv
            + jax.nn.sigmoid(z['gate_ret']) * y_ret) @ w_o


def _sqrelu_mlp(h, w_ff1, w_ff2):
    return jnp.square(jax.nn.relu(h @ w_ff1)) @ w_ff2


def setup_inputs(seed: int = 0) -> dict:
    key = jax.random.key(seed)
    ks = jax.random.split(key, 20)
    f32 = jnp.float32

    def nrm(k, shape, scale):
        return jax.random.normal(k, shape, f32) * scale

    gamma = 1.0 - 2.0 ** (-5.0 - np.arange(RET_HEADS))
    decay_logit = jnp.asarray(np.log(gamma / (1.0 - gamma)), dtype=f32)
    return {
        'x': nrm(ks[0], (BATCH, SEQ, D_MODEL), 1.0),
        'c': nrm(ks[1], (BATCH, D_MODEL), 1.0),
        'ctx': nrm(ks[2], (BATCH, CTX_LEN, D_MODEL), 1.0),
        'c_ctx': nrm(ks[3], (D_MODEL,), 1.0),
        'w_mod': nrm(ks[4], (DEPTH, D_MODEL, N_MOD * D_MODEL), 0.5 * D_MODEL ** -0.5),
        'b_mod': nrm(ks[5], (DEPTH, N_MOD * D_MODEL), 0.02),
        'norm1_g': 1.0 + nrm(ks[6], (DEPTH, D_MODEL), 0.02),
        'w_in': nrm(ks[7], (DEPTH, D_MODEL, D_IN), D_MODEL ** -0.5),
        'conv_w': nrm(ks[8], (DEPTH, CONV_WIDTH, D_CONV), CONV_WIDTH ** -0.5),
        'w_conv_out': nrm(ks[9], (DEPTH, D_CONV, D_MODEL), D_CONV ** -0.5),
        'ret_decay_fwd': decay_logit[None, :] + nrm(ks[10], (DEPTH, RET_HEADS), 0.1),
        'ret_decay_bwd': decay_logit[None, :] + nrm(ks[11], (DEPTH, RET_HEADS), 0.1),
        'w_ret_out': nrm(ks[12], (DEPTH, RET_HEADS * RET_DV, D_MODEL), (RET_HEADS * RET_DV) ** -0.5),
        'w_o': nrm(ks[13], (DEPTH, D_MODEL, D_MODEL), D_MODEL ** -0.5),
        'norm2_g': 1.0 + nrm(ks[14], (DEPTH, D_MODEL), 0.02),
        'w_ff1': nrm(ks[15], (DEPTH, D_MODEL, D_FF), D_MODEL ** -0.5),
        'w_ff2': nrm(ks[16], (DEPTH, D_FF, D_MODEL), D_FF ** -0.5),
        'final_g': 1.0 + nrm(ks[17], (D_MODEL,), 0.02),
    }


def reference(x, c, ctx, c_ctx, w_mod, b_mod, norm1_g, w_in, conv_w, w_conv_out,
              ret_decay_fwd, ret_decay_bwd, w_ret_out, w_o, norm2_g, w_ff1, w_ff2, final_g):
    b, seq, _ = x.shape
    rows = seq // GRID_W
    ctx_len = ctx.shape[1]
    pos_ctx = jnp.arange(ctx_len, dtype=jnp.float32)
    pos_lat = ctx_len + jnp.arange(seq, dtype=jnp.float32)
    h_ctx = ctx
    for layer in range(DEPTH):
        last = layer == DEPTH - 1
        mod_l = jnp.split(jax.nn.silu(c) @ w_mod[layer] + b_mod[layer], N_MOD, axis=-1)
        mod_c = jnp.split((jax.nn.silu(c_ctx) @ w_mod[layer] + b_mod[layer])[None], N_MOD, axis=-1)
        lg_f = jax.nn.log_sigmoid(ret_decay_fwd[layer].astype(jnp.float32))
        lg_b = jax.nn.log_sigmoid(ret_decay_bwd[layer].astype(jnp.float32))

        a_c = _modulate(h_ctx, norm1_g[layer], mod_c[0], mod_c[1])
        if last:
            zc = _combined_projection(a_c, w_in[layer], ('k', 'v'))
            kc = _rotary(_heads(zc['k'], RET_DK), pos_ctx) * RET_DK ** -0.5
            vc = _heads(zc['v'], RET_DV)
            st_f = _final_state(kc, vc, lg_f)
            st_b = _final_state(_flip(kc), _flip(vc), lg_b)
        else:
            zc = _combined_projection(a_c, w_in[layer], IN_NAMES)
            qc = _rotary(_heads(zc['q'], RET_DK), pos_ctx)
            kc = _rotary(_heads(zc['k'], RET_DK), pos_ctx) * RET_DK ** -0.5
            vc = _heads(zc['v'], RET_DV)
            zero = jnp.zeros((b, RET_HEADS, RET_DK, RET_DV), vc.dtype)
            oc_f, st_f = _retention_scan(qc, kc, vc, lg_f, zero)
            oc_b, st_b = _retention_scan(_flip(qc), _flip(kc), _flip(vc), lg_b, zero)
            y_c = _merge(zc,
                         _short_conv_branch(zc, conv_w[layer], w_conv_out[layer], None),
                         _retention_out(oc_f + _flip(oc_b), zc['g'], w_ret_out[layer]),
                         w_o[layer])
            h_ctx = h_ctx + mod_c[2][:, None, :] * y_c

        a_l = _modulate(x, norm1_g[layer], mod_l[0], mod_l[1])
        zl = _combined_projection(a_l, w_in[layer], IN_NAMES)
        ql = _rotary(_heads(zl['q'], RET_DK), pos_lat)
        kl = _rotary(_heads(zl['k'], RET_DK), pos_lat) * RET_DK ** -0.5
        vl = _heads(zl['v'], RET_DV)
        ol_f, _ = _retention_scan(ql, kl, vl, lg_f, st_f)
        ol_b, _ = _retention_scan(_flip(ql), _flip(kl), _flip(vl), lg_b, st_b)
        y_l = _merge(zl,
                     _short_conv_branch(zl, conv_w[layer], w_conv_out[layer], rows),
                     _retention_out(ol_f + _flip(ol_b), zl['g'], w_ret_out[layer]),
                     w_o[layer])
        x = x + mod_l[2][:, None, :] * y_l

        x = x + mod_l[5][:, None, :] * _sqrelu_mlp(
            _modulate(x, norm2_g[layer], mod_l[3], mod_l[4]), w_ff1[layer], w_ff2[layer])
        if not last:
            h_ctx = h_ctx + mod_c[5][:, None, :] * _sqrelu_mlp(
                _modulate(h_ctx, norm2_g[layer], mod_c[3], mod_c[4]), w_ff1[layer], w_ff2[layer])
    return _rmsnorm(x, final_g)
```

```python
import math
from contextlib import ExitStack

import numpy as np
import ml_dtypes

import concourse.bass as bass
import concourse.mybir as mybir
from concourse.bass_utils import run_bass_kernel_spmd

F32 = mybir.dt.float32
BF16 = mybir.dt.bfloat16
I32 = mybir.dt.int32
AF = mybir.ActivationFunctionType
ALU = mybir.AluOpType

D = 2048
NH = 8
DK = 256
DV = 512
DFF = 8192
CTX = 256
EPS = 1e-6
NMOD = 6
G = 512
OFF = {"conv_b": 0, "conv_c": 2048, "conv_x": 4096, "q": 6144, "k": 8192, "v": 10240,
       "g": 14336, "gate_conv": 18432, "gate_ret": 20480}
D_IN = 22528
TWO_PI_HI = 6.28125
TWO_PI_LO = 2.0 * math.pi - 6.28125


class Tok:
    __slots__ = ("eng", "sem", "val", "need", "dma")


class Buf:
    __slots__ = ("name", "w", "rs")

    def __init__(self, name=""):
        self.name = name
        self.w = []
        self.rs = {}


class DSem:
    __slots__ = ("sem", "count")

    def __init__(self, sem):
        self.sem = sem
        self.count = 0


class T:
    __slots__ = ("ap", "buf", "ds")

    def __init__(self, ap, name="", ds=None):
        self.ap = ap
        self.buf = Buf(name)
        self.ds = ds


ENGS = ("pe", "act", "dve", "pool", "sp")


class Rec:
    def __init__(self, nc, stack):
        self.nc = nc
        self.stack = stack
        self.ops = {e: [] for e in ENGS}
        self.pending = {e: [] for e in ENGS}
        self.last = {e: None for e in ENGS}
        self.dmas = {}
        self.nsem = 0

    def new_sem(self, name):
        self.nsem += 1
        return self.stack.enter_context(self.nc.semaphore(f"{name}_{self.nsem}"))

    def dsem(self, name="d"):
        return DSem(self.new_sem(name))

    def op(self, eng, fn, reads=(), writes=(), dsem=None, append=False, nobarrier=False):
        t = Tok()
        t.eng = eng
        t.dma = dsem is not None
        t.need = False
        t.sem = None
        t.val = None
        if t.dma:
            dsem.count += 16
            t.sem = dsem.sem
            t.val = dsem.count
            t.need = True
        waits = self.pending[eng]
        self.pending[eng] = []

        def dep(d, raw):
            if d.dma:
                waits.append(d)
                return
            if d.eng == eng and not t.dma:
                if eng == "pe" or not raw:
                    return
            d.need = True
            waits.append(d)

        for b in reads:
            for d in b.w:
                dep(d, True)
        if not append:
            for b in writes:
                for d in b.w:
                    dep(d, False)
                for d in b.rs.values():
                    dep(d, False)
        key = ("d", id(dsem)) if t.dma else eng
        for b in reads:
            b.rs[key] = t
        for b in writes:
            if append:
                b.w.append(t)
            else:
                b.w = [t]
                b.rs = {}
        self.ops[eng].append((fn, waits, t))
        if t.dma:
            if not nobarrier:
                self.dmas[id(dsem)] = t
        else:
            self.last[eng] = t
        return t

    def barrier(self):
        toks = [t for t in self.last.values() if t is not None] + list(self.dmas.values())
        for t in toks:
            if not t.dma:
                t.need = True
        for e in ENGS:
            self.pending[e] = self.pending[e] + toks
        self.dmas = {}

    def emit(self):
        nc = self.nc
        for e in ("pe", "act", "dve", "pool"):
            cnt = 0
            sem = None
            for fn, waits, t in self.ops[e]:
                if t.dma or not t.need:
                    continue
                if sem is None or cnt >= 30000:
                    sem = self.new_sem("c" + e)
                    cnt = 0
                cnt += 1
                t.sem = sem
                t.val = cnt

        def run(name, eng):
            waited = {}

            def do_waits(ws):
                for d in ws:
                    k = id(d.sem)
                    if waited.get(k, -1) >= d.val:
                        continue
                    eng.wait_ge(d.sem, d.val)
                    waited[k] = d.val

            for fn, waits, t in self.ops[name]:
                do_waits(waits)
                ins = fn(eng)
                if t.need:
                    ins.then_inc(t.sem, 16 if t.dma else 1)
            do_waits(self.pending[name])

        with nc.Block() as block:
            @block.tensor
            def _(eng):
                run("pe", eng)

            @block.scalar
            def _(eng):
                run("act", eng)

            @block.vector
            def _(eng):
                run("dve", eng)

            @block.gpsimd
            def _(eng):
                run("pool", eng)

            @block.sync
            def _(eng):
                run("sp", eng)


class Builder:
    def __init__(self, n_own, debug=False):
        self.N = n_own
        self.NC = n_own // 128
        self.NG = n_own // G
        self.TL = 2 * n_own + CTX
        self.debug = debug
        self.nc = bass.Bass("TRN2", target_bir_lowering=False)
        self.stack = ExitStack()
        self.R = Rec(self.nc, self.stack)
        self.bank_i = 0
        self.dspool = []
        self.dsidx = 0
        self.dsbase = 0

    def dram_in(self, name, shape, dt=F32):
        return self.nc.dram_tensor(name, list(shape), dt, kind="ExternalInput").ap()

    def dram_scratch(self, name, shape, dt, dbg=False):
        kind = "ExternalOutput" if (dbg and self.debug) else "Internal"
        return self.nc.dram_tensor(name, list(shape), dt, kind=kind).ap()

    def alloc(self, free_shape, dt, name="", ds=False):
        n = 1
        for s in free_shape:
            n *= s
        n16 = n * (2 if dt in (F32, I32) else 1)
        n16 = (n16 + 15) // 16 * 16
        assert self.aoff + n16 <= self.acap, f"SBUF arena overflow {name} {self.aoff}+{n16}>{self.acap}"
        ap = self.arena[:, self.aoff:self.aoff + n16]
        self.aoff += n16
        if dt in (F32, I32):
            ap = ap.bitcast(dt)
        ap = ap[:, 0:n]
        if len(free_shape) == 2:
            ap = ap.rearrange("p (a b) -> p a b", b=free_shape[1])
        elif len(free_shape) == 3:
            ap = ap.rearrange("p (a b c) -> p a b c", b=free_shape[1], c=free_shape[2])
        return T(ap, name, self.getds() if ds else None)

    def getds(self):
        if self.dsidx < len(self.dspool):
            d = self.dspool[self.dsidx]
        else:
            d = self.R.dsem("sb")
            self.dspool.append(d)
        self.dsidx += 1
        return d

    def bank(self):
        i = self.bank_i % 8
        self.bank_i += 1
        return self.PS[i]

    def dma(self, q, out_ap, in_ap, reads, writes, ds, append=False, nobarrier=False):
        return self.R.op(q, lambda e: e.dma_start(out=out_ap, in_=in_ap),
                         [r.buf for r in reads], [w.buf for w in writes], dsem=ds, append=append,
                         nobarrier=nobarrier)

    def pe(self, fn, reads, writes):
        return self.R.op("pe", fn, [r.buf for r in reads], [w.buf for w in writes])

    def act(self, fn, reads, writes):
        return self.R.op("act", fn, [r.buf for r in reads], [w.buf for w in writes])

    def dve(self, fn, reads, writes):
        return self.R.op("dve", fn, [r.buf for r in reads], [w.buf for w in writes])

    def mm(self, out_ap, lhsT, rhs, start, stop, reads, writes):
        return self.pe(lambda e: e.matmul(out_ap, lhsT=lhsT, rhs=rhs, start=start, stop=stop), reads, writes)

    def phase_begin(self):
        self.R.barrier()
        self.aoff = self.abase
        self.dsidx = self.dsbase

    class Stream:
        def __init__(self, B, units, slots, pre=None):
            self.B = B
            self.units = units
            self.slots = slots
            self.issued = 0
            self.pre = pre if pre is not None else len(slots) - 1

        def _issue(self):
            j = self.issued
            if j >= len(self.units):
                return
            dT, sl = self.units[j]
            slot = self.slots[j % len(self.slots)]
            self.B.dma("sp", sl(slot.ap), sl(dT.ap), [dT], [slot], slot.ds)
            self.issued += 1

        def get(self, j):
            while self.issued <= min(j + self.pre, len(self.units) - 1):
                self._issue()
            return self.slots[j % len(self.slots)]

    def build(self):
        nc, R, N, NC, NG, TL = self.nc, self.R, self.N, self.NC, self.NG, self.TL
        st = self.stack
        xall = self.dram_in("xall", [TL, D])
        pos = self.dram_in("pos", [TL])
        cvec = self.dram_in("cvec", [D])
        cctx = self.dram_in("cctx", [D])
        w_mod = self.dram_in("w_mod", [D, NMOD * D])
        b_mod = self.dram_in("b_mod", [NMOD * D])
        g1 = self.dram_in("g1", [D])
        g2 = self.dram_in("g2", [D])
        fg = self.dram_in("fg", [D])
        w_in = self.dram_in("w_in", [D, D_IN])
        conv_w = self.dram_in("conv_w", [3, D])
        w_co = self.dram_in("w_co", [D, D])
        decF = self.dram_in("decF", [NH])
        decB = self.dram_in("decB", [NH])
        w_ro = self.dram_in("w_ro", [NH * DV, D])
        w_o = self.dram_in("w_o", [D, D])
        w_f1 = self.dram_in("w_f1", [D, DFF])
        w_f2 = self.dram_in("w_f2", [DFF, D])
        cst = self.dram_in("cst", [128, 7 * 128 + 2])
        invf = self.dram_in("invf", [128])
        out = self.nc.dram_tensor("out", [N, D], F32, kind="ExternalOutput").ap()
        NCH = 2 * NC + 2
        WIN = self.dram_scratch("WIN", [52, 128, 16, 512], BF16)
        WRO = self.dram_scratch("WRO", [16, 128, 32, 128], BF16)
        WO = self.dram_scratch("WO", [4, 128, 16, 512], BF16)
        WF1 = self.dram_scratch("WF1", [32, 128, 16, 256], BF16)
        WF2 = self.dram_scratch("WF2", [4, 8, 128, 8, 512], BF16)
        QT = self.dram_scratch("QT", [NC, 128, 16, 128], BF16, dbg=True)
        KT = self.dram_scratch("KT", [NCH, 128, 16, 128], BF16, dbg=True)
        VS = self.dram_scratch("VS", [TL, NH * DV], BF16, dbg=True)
        GT = self.dram_scratch("GT", [NC, 128, 32, 128], BF16, dbg=True)
        SGR = self.dram_scratch("SGR", [16, 128, N], BF16, dbg=True)
        MCS = self.dram_scratch("MCS", [16, 128, N], F32, dbg=True)
        SBS = self.dram_scratch("SBS", [NC, 128, 16, 512], BF16, dbg=True)
        GON = self.dram_scratch("GON", [NC, 128, 32, 128], BF16, dbg=True)
        X1 = self.dram_scratch("X1", [N, D], F32, dbg=True)
        A2T = self.dram_scratch("A2T", [NG, 128, 16, G], BF16, dbg=True)
        BCS = self.dram_scratch("BCS", [2, 128, D], F32)
        self.acap = 212000 // 2 // 16 * 16
        self.arena = st.enter_context(nc.sbuf_tensor("arena", [128, self.acap], BF16))
        self.aoff = 0
        self.PS = []
        for i in range(8):
            p = st.enter_context(nc.psum_tensor(f"ps{i}", [128, 512], F32))
            self.PS.append(T(p[:, :], f"ps{i}"))

        dma, pe, act, dve, mm = self.dma, self.pe, self.act, self.dve, self.mm
        B = self

        cvt = []
        winT = [T(WIN[u], f"win{u}") for u in range(52)]
        conv_order = list(range(28, 40)) + list(range(0, 28)) + list(range(40, 52))
        grp_of = {}
        for i_, u_ in enumerate(conv_order):
            grp_of[u_] = (i_ // 4) if i_ < 12 else 3 + (i_ - 12) // 8
        win_grp_ds = [R.dsem(f"wing{i}") for i in range(8)]
        win_ds = [win_grp_ds[grp_of[u]] for u in range(52)]

        def conv_cols(u, src_ap_cols, c0, ncols, K=D, kcn=16, dstT=None, dst=None, ds=None):
            for q in range(0, kcn, 4):
                src = src_ap_cols[q * 128:(q + 4) * 128, :].rearrange("(kc p) c -> p kc c", p=128)
                cvt.append(dma("pool", dst[:, q:q + 4, c0:c0 + ncols], src, [], [dstT], ds, append=True,
                               nobarrier=True))

        def win_unit_cols(u):
            if u < 16:
                ct = u
                return [(0, 128, w_in[:, OFF["conv_c"] + ct * 128: OFF["conv_c"] + (ct + 1) * 128]),
                        (128, 128, w_in[:, OFF["conv_x"] + ct * 128: OFF["conv_x"] + (ct + 1) * 128]),
                        (256, 128, w_in[:, OFF["conv_b"] + ct * 128: OFF["conv_b"] + (ct + 1) * 128])]
            if u < 20:
                o = OFF["gate_conv"] + (u - 16) * 512
            elif u < 24:
                return [(0, 512, w_co[:, (u - 20) * 512:(u - 19) * 512])]
            elif u < 28:
                o = OFF["q"] + (u - 24) * 512
            elif u < 32:
                o = OFF["k"] + (u - 28) * 512
            elif u < 40:
                o = OFF["v"] + (u - 32) * 512
            elif u < 48:
                o = OFF["g"] + (u - 40) * 512
            else:
                o = OFF["gate_ret"] + (u - 48) * 512
            return [(0, 512, w_in[:, o:o + 512])]

        for u in conv_order:
            for (c0, ncols, src) in win_unit_cols(u):
                conv_cols(u, src, c0, ncols, dstT=winT[u], dst=WIN[u], ds=win_ds[u])
        wroT = [T(WRO[f], f"wro{f}") for f in range(16)]
        ds_ro = R.dsem("wro")
        for f in range(16):
            for q in range(0, 32, 4):
                src = w_ro[q * 128:(q + 4) * 128, f * 128:(f + 1) * 128].rearrange("(kc p) c -> p kc c", p=128)
                cvt.append(dma("pool", WRO[f][:, q:q + 4, :], src, [], [wroT[f]], ds_ro, append=True,
                               nobarrier=True))
        woT = [T(WO[c], f"wo{c}") for c in range(4)]
        ds_wo = R.dsem("wo")
        for c in range(4):
            conv_cols(0, w_o[:, c * 512:(c + 1) * 512], 0, 512, dstT=woT[c], dst=WO[c], ds=ds_wo)
        wf1T = [T(WF1[u], f"wf1{u}") for u in range(32)]
        ds_f1 = R.dsem("wf1")
        for u in range(32):
            conv_cols(0, w_f1[:, u * 256:(u + 1) * 256], 0, 256, dstT=wf1T[u], dst=WF1[u], ds=ds_f1)
        wf2T = [[T(WF2[c, s], f"wf2{c}_{s}") for s in range(8)] for c in range(4)]
        ds_f2 = R.dsem("wf2")
        for c in range(4):
            for s in range(8):
                for q in range(0, 8, 4):
                    r0 = (s * 8 + q) * 128
                    src = w_f2[r0:r0 + 512, c * 512:(c + 1) * 512].rearrange("(kc p) c -> p kc c", p=128)
                    cvt.append(dma("pool", WF2[c, s][:, q:q + 4, :], src, [], [wf2T[c][s]], ds_f2, append=True,
                                   nobarrier=True))
        for t_ in cvt:
            for ds_ in [ds_ro, ds_wo, ds_f1, ds_f2] + win_grp_ds:
                if t_.sem is ds_.sem:
                    t_.val = ds_.count

        CT = self.alloc([7 * 128 + 2], F32, "CT", ds=True)
        dma("sp", CT.ap, cst, [], [CT], CT.ds)
        ident = CT.ap[:, 0:128]
        relp = CT.ap[:, 128:256]
        reln = CT.ap[:, 256:384]
        mge = CT.ap[:, 384:512]
        mle = CT.ap[:, 512:640]
        fi1 = CT.ap[:, 640:768]
        fi2 = CT.ap[:, 768:896]
        pF = CT.ap[:, 896:897]
        pB = CT.ap[:, 897:898]
        identb = self.alloc([128], BF16, "identb")
        dve(lambda e: e.tensor_copy(out=identb.ap, in_=ident), [CT], [identb])
        invfT = self.alloc([1], F32, "invf", ds=True)
        dma("sp", invfT.ap, invf.rearrange("(p o) -> p o", o=1), [], [invfT], invfT.ds)
        CC = self.alloc([4], F32, "CC")
        dve(lambda e: e.memset(CC.ap[:, 0:1], EPS), [], [CC])
        dve(lambda e: e.memset(CC.ap[:, 1:2], math.pi / 2), [CC], [CC])
        epsc = CC.ap[:, 0:1]
        hpic = CC.ap[:, 1:2]
        VC = self.alloc([128], F32, "VC")
        BC = self.alloc([96], F32, "BC")
        MODC = self.alloc([4, 16, 2], F32, "MODC")
        GS = self.alloc([6, 16], F32, "GS")
        LG = self.alloc([2, 8], F32, "LG")
        KDEC = self.alloc([2, 8], F32, "KDEC")
        CD = self.alloc([2, 8], F32, "CD")
        self.abase = self.aoff
        self.dsbase = self.dsidx

        VR = self.alloc([128], F32, "VR", ds=True)
        BR = self.alloc([128], F32, "BR", ds=True)
        dve(lambda e: e.memset(VR.ap, 0.0), [], [VR])
        dve(lambda e: e.memset(BR.ap, 0.0), [], [BR])
        dma("sp", VR.ap[0:16, :], cvec.rearrange("(a b) -> a b", b=128), [], [VR], VR.ds)
        dma("sp", VR.ap[16:32, :], cctx.rearrange("(a b) -> a b", b=128), [VR], [VR], VR.ds, append=True)
        dma("sp", VR.ap[32:48, :], g1.rearrange("(a b) -> a b", b=128), [VR], [VR], VR.ds, append=True)
        dma("sp", VR.ap[48:64, :], g2.rearrange("(a b) -> a b", b=128), [VR], [VR], VR.ds, append=True)
        dma("sp", VR.ap[64:112, :], conv_w.rearrange("t (a b) -> (t a) b", b=128), [VR], [VR], VR.ds, append=True)
        dma("sp", BR.ap[0:96, :], b_mod.rearrange("(a b) -> a b", b=128), [], [BR], BR.ds)
        bk = self.bank()
        pe(lambda e: e.transpose(out=bk.ap[:, 0:128], in_=VR.ap, identity=ident), [VR, CT], [bk])
        dve(lambda e: e.tensor_copy(out=VC.ap, in_=bk.ap[:, 0:128]), [bk], [VC])
        bk2 = self.bank()
        pe(lambda e: e.transpose(out=bk2.ap[:, 0:128], in_=BR.ap, identity=ident), [BR, CT], [bk2])
        dve(lambda e: e.tensor_copy(out=BC.ap, in_=bk2.ap[:, 0:96]), [bk2], [BC])
        SC = self.alloc([16, 2], F32, "SC")
        act(lambda e: e.activation(out=SC.ap[:, :, 0], in_=VC.ap[:, 0:16], func=AF.Silu), [VC], [SC])
        act(lambda e: e.activation(out=SC.ap[:, :, 1], in_=VC.ap[:, 16:32], func=AF.Silu), [VC, SC], [SC])
        SCB = self.alloc([16, 128], F32, "SCB")
        dve(lambda e: e.tensor_copy(out=SCB.ap, in_=SC.ap[:, :, 0:1].to_broadcast([128, 16, 128])), [SC], [SCB])
        WM = [self.alloc([16, 512], F32, f"WM{i}", ds=True) for i in range(2)]
        pieces = []
        for m in (0, 1, 3, 4, 2, 5):
            for pc in range(4):
                pieces.append((m, pc))
        BCr = [self.alloc([D], F32, f"BCr{i}", ds=True) for i in range(2)]
        bmr = self.alloc([512], F32, "bmr", ds=True)
        BCST = [T(BCS[i], f"bcs{i}") for i in range(2)]

        def wm_load(i):
            m, pc = pieces[i]
            s = WM[i % 2]
            src = w_mod[:, m * D + pc * 512: m * D + (pc + 1) * 512].rearrange("(kc p) c -> p kc c", p=128)
            dma("sp", s.ap, src, [], [s], s.ds)

        wm_load(0)
        for i, (m, pc) in enumerate(pieces):
            if i + 1 < len(pieces):
                wm_load(i + 1)
            s = WM[i % 2]
            if m in (0, 1, 3, 4):
                mi = (0, 1, 3, 4).index(m)
                bk = self.bank()
                for dtl in range(4):
                    for kc in range(16):
                        mm(bk.ap[:, dtl * 2:dtl * 2 + 2], s.ap[:, kc, dtl * 128:(dtl + 1) * 128], SC.ap[:, kc, :],
                           kc == 0, kc == 15, [s, SC], [bk])
                for dtl in range(4):
                    dt_ = pc * 4 + dtl
                    dve(lambda e, bk=bk, dtl=dtl, mi=mi, dt_=dt_, m=m: e.tensor_tensor(
                        out=MODC.ap[:, mi, dt_, :], in0=bk.ap[:, dtl * 2:dtl * 2 + 2],
                        in1=BC.ap[:, m * 16 + dt_: m * 16 + dt_ + 1].to_broadcast([128, 2]), op=ALU.add),
                        [bk, BC, MODC], [MODC])
            else:
                j = 0 if m == 2 else 1
                bk = self.bank()
                for kc in range(16):
                    mm(bk.ap, SCB.ap[:, kc, :], s.ap[:, kc, :], kc == 0, kc == 15, [s, SCB], [bk])
                dma("sp", bmr.ap, b_mod[m * D + pc * 512: m * D + (pc + 1) * 512].partition_broadcast(128),
                    [], [bmr], bmr.ds)
                dve(lambda e, bk=bk, j=j, pc=pc: e.tensor_tensor(
                    out=BCr[j].ap[:, pc * 512:(pc + 1) * 512], in0=bk.ap, in1=bmr.ap, op=ALU.add),
                    [bk, bmr, BCr[j]], [BCr[j]])
                if pc == 3:
                    dma("sp", BCS[j], BCr[j].ap, [BCr[j]], [BCST[j]], BCr[j].ds)
        def gs_make(dst_g, dst_s, mi_shift, mi_scale, which, gcol0):
            dve(lambda e: e.scalar_tensor_tensor(out=GS.ap[:, dst_g, :], in0=MODC.ap[:, mi_scale, :, which],
                                                 scalar=1.0, in1=VC.ap[:, gcol0:gcol0 + 16],
                                                 op0=ALU.add, op1=ALU.mult), [MODC, VC, GS], [GS])
            dve(lambda e: e.tensor_copy(out=GS.ap[:, dst_s, :], in_=MODC.ap[:, mi_shift, :, which]),
                [MODC, GS], [GS])

        gs_make(0, 1, 0, 1, 0, 32)
        gs_make(2, 3, 0, 1, 1, 32)
        gs_make(4, 5, 2, 3, 0, 48)
        dT_ = self.alloc([2, 8], F32, "decin", ds=True)
        dma("sp", dT_.ap[:, 0, :], decF.partition_broadcast(128), [], [dT_], dT_.ds)
        dma("sp", dT_.ap[:, 1, :], decB.partition_broadcast(128), [dT_], [dT_], dT_.ds, append=True)
        tmpe = self.alloc([2, 8], F32, "tmpe")
        act(lambda e: e.activation(out=tmpe.ap, in_=dT_.ap, func=AF.Exp, scale=-1.0), [dT_], [tmpe])
        act(lambda e: e.activation(out=tmpe.ap, in_=tmpe.ap, func=AF.Ln, bias=1.0), [tmpe], [tmpe])
        dve(lambda e: e.tensor_scalar(out=LG.ap, in0=tmpe.ap, scalar1=-1.0, scalar2=None, op0=ALU.mult),
            [tmpe], [LG])
        act(lambda e: e.activation(out=KDEC.ap[:, 0, :], in_=LG.ap[:, 0, :], func=AF.Exp, scale=pF), [LG, CT], [KDEC])
        act(lambda e: e.activation(out=KDEC.ap[:, 1, :], in_=LG.ap[:, 1, :], func=AF.Exp, scale=pB), [LG, CT, KDEC], [KDEC])
        dve(lambda e: e.tensor_scalar(out=KDEC.ap, in0=KDEC.ap, scalar1=1.0 / 16.0, scalar2=None, op0=ALU.mult),
            [KDEC], [KDEC])
        act(lambda e: e.activation(out=CD.ap, in_=LG.ap, func=AF.Exp, scale=128.0), [LG], [CD])

        self.phase_begin()
        XT = [self.alloc([D], F32, f"xt{i}", ds=True) for i in range(2)]
        XNs = [self.alloc([D], F32, f"xn{i}") for i in range(2)]
        SS = [self.alloc([4], F32, f"ss{i}") for i in range(2)]
        AT = [self.alloc([16, G], BF16, f"aT{i}") for i in range(2)]
        RING = [self.alloc([16, 512], BF16, f"ring{i}", ds=True) for i in range(3)]
        POSB = self.alloc([G], F32, "posb", ds=True)
        ANG = self.alloc([G], F32, "ang")
        KI = self.alloc([G], I32, "ki")
        KF = self.alloc([G], F32, "kf")
        RR = self.alloc([G], F32, "rr")
        RC = self.alloc([G], F32, "rc")
        COS = self.alloc([G], F32, "cos")
        SIN = self.alloc([G], F32, "sin")
        YBC = self.alloc([16, G], BF16, "ybc")
        CSB = [self.alloc([G], F32, f"csb{i}") for i in range(2)]
        UP = [self.alloc([G // 64, 66], F32, f"up{i}") for i in range(2)]
        YC = [self.alloc([G], F32, f"yc{i}") for i in range(2)]
        RT = [self.alloc([G], F32, f"rt{i}") for i in range(4)]
        RO = [self.alloc([G], BF16, f"ro{i}", ds=True) for i in range(4)]
        SG = self.alloc([4, G], BF16, "sg")
        MCO = [self.alloc([G], F32, f"mco{i}", ds=True) for i in range(2)]
        VO = [self.alloc([512], BF16, f"vo{i}", ds=True) for i in range(3)]
        GO = [self.alloc([G], BF16, f"go{i}", ds=True) for i in range(3)]
        for u_ in UP:
            dve(lambda e, u_=u_: e.memset(u_.ap, 0.0), [], [u_])

        groups = []
        groups.append(("kv", 2 * N, CTX, 1))
        for g in range(NG):
            groups.append(("kv", N + g * G, G, 0))
        for g in range(NG):
            groups.append(("full", g * G, G, 0))
        full_units = list(range(0, 16))
        for fq in range(4):
            full_units += [16 + fq, 20 + fq]
        full_units += list(range(24, 52))
        kv_units = list(range(28, 40))
        unit_seq = []
        for (kind, tok0, T_, ms) in groups:
            unit_seq += kv_units if kind == "kv" else full_units
        ktT = [T(KT[c], f"kt{c}") for c in range(NCH)]
        vsT = [T(VS[c * 128:(c + 1) * 128, :], f"vs{c}") for c in range(NCH)]
        qtT = [T(QT[c], f"qt{c}") for c in range(NC)]
        gtT = [T(GT[c], f"gt{c}") for c in range(NC)]
        sgrT = [[T(SGR[f, :, g * G:(g + 1) * G], f"sgr{f}_{g}") for g in range(NG)] for f in range(16)]
        mcsT = [[T(MCS[f, :, g * G:(g + 1) * G], f"mcs{f}_{g}") for g in range(NG)] for f in range(16)]

        def unit_slice(u):
            nco = 384 if u < 16 else 512
            return lambda sap: sap[:, :, 0:nco]
        stream = self.Stream(self, [(winT[u], unit_slice(u)) for u in unit_seq], RING)
        ucount = [0]
        rot_i = [0]
        cnt = {"xt": 0, "ro": 0, "mco": 0, "vo": 0, "go": 0, "cv": 0}

        def norm_tile(gidx, tt):
            kind, tok0, T_, ms = groups[gidx]
            if tt >= T_ // 128:
                return
            aT = AT[gidx % 2]
            gcol, scol = (2, 3) if ms == 1 else (0, 1)
            i2 = cnt["xt"] % 2
            xt, ss, xn = XT[i2], SS[i2], XNs[i2]
            cnt["xt"] += 1
            r0 = tok0 + tt * 128
            dma("sp", xt.ap, xall[r0:r0 + 128, :], [], [xt], xt.ds)
            dve(lambda e: e.memset(ss.ap[:, 0:1], 0.0), [], [ss])
            act(lambda e: e.activation(out=xn.ap, in_=xt.ap, func=AF.Square, accum_out=ss.ap[:, 0:1]),
                [xt], [xn, ss])
            act(lambda e: e.activation(out=ss.ap[:, 1:2], in_=ss.ap[:, 0:1], func=AF.Sqrt, scale=1.0 / D,
                                       bias=epsc), [ss, CC], [ss])
            dve(lambda e: e.reciprocal(out=ss.ap[:, 2:3], in_=ss.ap[:, 1:2]), [ss], [ss])
            act(lambda e: e.activation(out=xn.ap, in_=xt.ap, func=AF.Copy, scale=ss.ap[:, 2:3]), [xt, ss], [xn])
            for q4 in range(4):
                bk = self.bank()
                for j in range(4):
                    dt_ = q4 * 4 + j
                    pe(lambda e, bk=bk, j=j, dt_=dt_: e.transpose(
                        out=bk.ap[:, j * 128:(j + 1) * 128], in_=xn.ap[:, dt_ * 128:(dt_ + 1) * 128],
                        identity=ident), [xn, CT], [bk])
                for j in range(4):
                    dt_ = q4 * 4 + j
                    act(lambda e, bk=bk, j=j, dt_=dt_: e.activation(
                        out=aT.ap[:, dt_, tt * 128:(tt + 1) * 128], in_=bk.ap[:, j * 128:(j + 1) * 128],
                        func=AF.Identity, scale=GS.ap[:, gcol, dt_:dt_ + 1], bias=GS.ap[:, scol, dt_:dt_ + 1]),
                        [bk, GS], [aT])

        def fm_tile(aT, T_, slot, c0):
            bk = self.bank()
            for kc in range(16):
                mm(bk.ap[:, 0:T_], slot.ap[:, kc, c0:c0 + 128], aT.ap[:, kc, 0:T_], kc == 0, kc == 15,
                   [slot, aT], [bk])
            return bk

        def rope_tables(tok0, T_):
            dma("sp", POSB.ap[:, 0:T_], pos[tok0:tok0 + T_].partition_broadcast(128), [], [POSB], POSB.ds)
            dve(lambda e: e.tensor_scalar(out=ANG.ap[:, 0:T_], in0=POSB.ap[:, 0:T_], scalar1=invfT.ap[:, 0:1],
                                          scalar2=None, op0=ALU.mult), [POSB, invfT], [ANG])
            dve(lambda e: e.tensor_scalar(out=KF.ap[:, 0:T_], in0=ANG.ap[:, 0:T_], scalar1=1.0 / (2 * math.pi),
                                          scalar2=None, op0=ALU.mult), [ANG], [KF])
            dve(lambda e: e.tensor_copy(out=KI.ap[:, 0:T_], in_=KF.ap[:, 0:T_]), [KF], [KI])
            dve(lambda e: e.tensor_copy(out=KF.ap[:, 0:T_], in_=KI.ap[:, 0:T_]), [KI], [KF])
            dve(lambda e: e.scalar_tensor_tensor(out=RR.ap[:, 0:T_], in0=KF.ap[:, 0:T_], scalar=-TWO_PI_HI,
                                                 in1=ANG.ap[:, 0:T_], op0=ALU.mult, op1=ALU.add), [KF, ANG], [RR])
            dve(lambda e: e.scalar_tensor_tensor(out=RR.ap[:, 0:T_], in0=KF.ap[:, 0:T_], scalar=-TWO_PI_LO,
                                                 in1=RR.ap[:, 0:T_], op0=ALU.mult, op1=ALU.add), [KF, RR], [RR])
            dve(lambda e: e.tensor_scalar(out=RR.ap[:, 0:T_], in0=RR.ap[:, 0:T_], scalar1=-3.14159,
                                          scalar2=3.14159, op0=ALU.max, op1=ALU.min), [RR], [RR])
            act(lambda e: e.activation(out=SIN.ap[:, 0:T_], in_=RR.ap[:, 0:T_], func=AF.Sin), [RR], [SIN])
            act(lambda e: e.activation(out=RC.ap[:, 0:T_], in_=RR.ap[:, 0:T_], func=AF.Abs), [RR], [RC])
            act(lambda e: e.activation(out=COS.ap[:, 0:T_], in_=RC.ap[:, 0:T_], func=AF.Sin, scale=-1.0,
                                       bias=hpic), [RC, CC], [COS])

        def rope_pair(aT, T_, slot, c0, dstT_list, tile0, chunk0):
            b1 = fm_tile(aT, T_, slot, c0)
            b2 = fm_tile(aT, T_, slot, c0 + 128)
            ta, tb, tc, td = RT
            dve(lambda e: e.tensor_tensor(out=ta.ap[:, 0:T_], in0=b1.ap[:, 0:T_], in1=COS.ap[:, 0:T_], op=ALU.mult),
                [b1, COS], [ta])
            dve(lambda e: e.tensor_tensor(out=tb.ap[:, 0:T_], in0=b2.ap[:, 0:T_], in1=SIN.ap[:, 0:T_], op=ALU.mult),
                [b2, SIN], [tb])
            dve(lambda e: e.tensor_tensor(out=tc.ap[:, 0:T_], in0=b1.ap[:, 0:T_], in1=SIN.ap[:, 0:T_], op=ALU.mult),
                [b1, SIN], [tc])
            dve(lambda e: e.tensor_tensor(out=td.ap[:, 0:T_], in0=b2.ap[:, 0:T_], in1=COS.ap[:, 0:T_], op=ALU.mult),
                [b2, COS], [td])
            r1 = RO[cnt["ro"] % 4]
            r2 = RO[(cnt["ro"] + 1) % 4]
            cnt["ro"] += 2
            dve(lambda e: e.tensor_tensor(out=r1.ap[:, 0:T_], in0=ta.ap[:, 0:T_], in1=tb.ap[:, 0:T_], op=ALU.subtract),
                [ta, tb], [r1])
            dve(lambda e: e.tensor_tensor(out=r2.ap[:, 0:T_], in0=tc.ap[:, 0:T_], in1=td.ap[:, 0:T_], op=ALU.add),
                [tc, td], [r2])
            nchk = T_ // 128
            for r, tl in ((r1, tile0), (r2, tile0 + 1)):
                for cc in range(nchk):
                    dT = dstT_list[chunk0 + cc]
                    dma("sp", dT.ap[:, tl, :], r.ap[:, cc * 128:(cc + 1) * 128], [r], [dT], r.ds, append=True)

        gi_own = 0
        for tt in range(4):
            norm_tile(0, tt)
        rope_tables(groups[0][1], groups[0][2])
        for gidx, (kind, tok0, T_, ms) in enumerate(groups):
            aT = AT[gidx % 2]
            chunk0 = tok0 // 128
            units = kv_units if kind == "kv" else full_units
            nu = len(units)
            sched = {}
            if gidx + 1 < len(groups):
                last_rope = 3 if kind == "kv" else 31
                for tt in range(4):
                    pos_ = (last_rope + 1 + tt * ((nu - last_rope - 2) // 4)) if kind == "full" else (4 + 2 * tt)
                    sched.setdefault(min(pos_, nu - 1), []).append(("norm", tt))
                sched.setdefault(last_rope, []).insert(0, ("rope", 0))
            for ui, u in enumerate(units):
                for (what, tt) in sched.get(ui - 1, []) if ui > 0 else []:
                    if what == "norm":
                        norm_tile(gidx + 1, tt)
                    else:
                        rope_tables(groups[gidx + 1][1], groups[gidx + 1][2])
                slot = stream.get(ucount[0])
                ucount[0] += 1
                if u < 16:
                    ct = u
                    bc_ = fm_tile(aT, T_, slot, 0)
                    bx_ = fm_tile(aT, T_, slot, 128)
                    bb_ = fm_tile(aT, T_, slot, 256)
                    i2 = cnt["cv"] % 2
                    cnt["cv"] += 1
                    csb, up, yc = CSB[i2], UP[i2], YC[i2]
                    act(lambda e, bc_=bc_, csb=csb: e.activation(out=csb.ap, in_=bc_.ap, func=AF.Copy), [bc_], [csb])
                    dve(lambda e, up=up, csb=csb, bx_=bx_: e.tensor_tensor(
                        out=up.ap[:, :, 1:65], in0=csb.ap.rearrange("p (r w) -> p r w", w=64),
                        in1=bx_.ap.rearrange("p (r w) -> p r w", w=64), op=ALU.mult), [csb, bx_], [up])
                    ycv = yc.ap.rearrange("p (r w) -> p r w", w=64)
                    dve(lambda e, up=up, ycv=ycv, ct=ct: e.tensor_scalar(
                        out=ycv, in0=up.ap[:, :, 1:65], scalar1=VC.ap[:, 64 + 16 + ct:64 + 16 + ct + 1],
                        scalar2=None, op0=ALU.mult), [up, VC], [yc])
                    dve(lambda e, up=up, ycv=ycv, ct=ct: e.scalar_tensor_tensor(
                        out=ycv, in0=up.ap[:, :, 0:64], scalar=VC.ap[:, 64 + ct:64 + ct + 1], in1=ycv,
                        op0=ALU.mult, op1=ALU.add), [up, VC, yc], [yc])
                    dve(lambda e, up=up, ycv=ycv, ct=ct: e.scalar_tensor_tensor(
                        out=ycv, in0=up.ap[:, :, 2:66], scalar=VC.ap[:, 64 + 32 + ct:64 + 32 + ct + 1], in1=ycv,
                        op0=ALU.mult, op1=ALU.add), [up, VC, yc], [yc])
                    dve(lambda e, yc=yc, bb_=bb_, ct=ct: e.tensor_tensor(
                        out=YBC.ap[:, ct, :], in0=yc.ap, in1=bb_.ap, op=ALU.mult), [yc, bb_], [YBC])
                elif u < 20:
                    for j in range(4):
                        bk = fm_tile(aT, T_, slot, j * 128)
                        act(lambda e, bk=bk, j=j: e.activation(out=SG.ap[:, j, :], in_=bk.ap, func=AF.Sigmoid),
                            [bk], [SG])
                elif u < 24:
                    fq = u - 20
                    for j in range(4):
                        ft = fq * 4 + j
                        bk = self.bank()
                        for kc in range(16):
                            mm(bk.ap, slot.ap[:, kc, j * 128:(j + 1) * 128], YBC.ap[:, kc, :], kc == 0, kc == 15,
                               [slot, YBC], [bk])
                        mo = MCO[cnt["mco"] % 2]
                        cnt["mco"] += 1
                        dve(lambda e, bk=bk, j=j, mo=mo: e.tensor_tensor(out=mo.ap, in0=bk.ap, in1=SG.ap[:, j, :],
                                                                          op=ALU.mult), [bk, SG], [mo])
                        dT = mcsT[ft][gi_own]
                        dma("sp", dT.ap, mo.ap, [mo], [dT], mo.ds)
                elif u < 28:
                    b_ = u - 24
                    for hh in range(2):
                        rope_pair(aT, T_, slot, hh * 256, qtT, (b_ * 2 + hh) * 2, chunk0)
                elif u < 32:
                    b_ = u - 28
                    for hh in range(2):
                        rope_pair(aT, T_, slot, hh * 256, ktT, (b_ * 2 + hh) * 2, chunk0)
                elif u < 40:
                    vb = u - 32
                    for tt in range(T_ // 128):
                        bk = self.bank()
                        for kc in range(16):
                            mm(bk.ap, aT.ap[:, kc, tt * 128:(tt + 1) * 128], slot.ap[:, kc, :], kc == 0, kc == 15,
                               [slot, aT], [bk])
                        vo = VO[cnt["vo"] % 3]
                        cnt["vo"] += 1
                        act(lambda e, bk=bk, vo=vo: e.activation(out=vo.ap, in_=bk.ap, func=AF.Copy), [bk], [vo])
                        dT = vsT[chunk0 + tt]
                        dma("sp", dT.ap[:, vb * 512:(vb + 1) * 512], vo.ap, [vo], [dT], vo.ds, append=True)
                elif u < 48:
                    gb = u - 40
                    for j in range(4):
                        gt_ = gb * 4 + j
                        bk = fm_tile(aT, T_, slot, j * 128)
                        go = GO[cnt["go"] % 3]
                        cnt["go"] += 1
                        act(lambda e, bk=bk, go=go: e.activation(out=go.ap, in_=bk.ap, func=AF.Silu), [bk], [go])
                        for cc in range(T_ // 128):
                            dT = gtT[chunk0 + cc]
                            dma("sp", dT.ap[:, gt_, :], go.ap[:, cc * 128:(cc + 1) * 128], [go], [dT], go.ds,
                                append=True)
                else:
                    fq = u - 48
                    for j in range(4):
                        ft = fq * 4 + j
                        bk = fm_tile(aT, T_, slot, j * 128)
                        go = GO[cnt["go"] % 3]
                        cnt["go"] += 1
                        act(lambda e, bk=bk, go=go: e.activation(out=go.ap, in_=bk.ap, func=AF.Sigmoid), [bk], [go])
                        dT = sgrT[ft][gi_own]
                        dma("sp", dT.ap, go.ap, [go], [dT], go.ds)
            for (what, tt) in sched.get(nu - 1, []):
                if what == "norm":
                    norm_tile(gidx + 1, tt)
                else:
                    rope_tables(groups[gidx + 1][1], groups[gidx + 1][2])
            if kind == "full":
                gi_own += 1

        self.phase_begin()
        ktT_, vsT_ = ktT, vsT
        sbsT = [T(SBS[c], f"sbs{c}") for c in range(NC)]
        gonT = [T(GON[c], f"gon{c}") for c in range(NC)]
        cntS = {"kv": 0, "ktl": 0, "sbf": 0}
        pool = lambda fn, reads, writes: self.R.op("pool", fn, [r.buf for r in reads], [w.buf for w in writes])

        def sweep_bufs():
            S32 = [self.alloc([512], F32, f"s32_{i}") for i in range(16)]
            KTs = [self.alloc([16, 128], BF16, f"kts{i}", ds=True) for i in range(2)]
            VTs = [self.alloc([NH * DV], BF16, f"vts{i}", ds=True) for i in range(2)]
            KTL = [self.alloc([8, 256], BF16, f"ktl{i}") for i in range(2)]
            return S32, KTs, VTs, KTL

        def load_kv(c):
            i = cntS["kv"] % 2
            cntS["kv"] += 1
            kts, vts = KTs[i], VTs[i]
            dma("sp", kts.ap, ktT[c].ap, [ktT[c]], [kts], kts.ds)
            dma("sp", vts.ap, vsT[c].ap, [vsT[c]], [vts], vts.ds)
            return kts, vts

        def k_tilde(kts, di):
            ktl = KTL[cntS["ktl"] % 2]
            cntS["ktl"] += 1
            bks = [self.bank(), self.bank()]
            for j in range(16):
                bkv = bks[j // 8].ap.bitcast(BF16)
                pe(lambda e, bkv=bkv, j=j: e.transpose(out=bkv[:, (j % 8) * 128:(j % 8 + 1) * 128],
                                                       in_=kts.ap[:, j, :], identity=identb.ap),
                   [kts, identb], [bks[j // 8]])
            for h in range(8):
                bkv = bks[h // 4].ap.bitcast(BF16)
                act(lambda e, bkv=bkv, h=h: e.activation(out=ktl.ap[:, h, :],
                                                         in_=bkv[:, (h % 4) * 256:(h % 4 + 1) * 256],
                                                         func=AF.Copy, scale=KDEC.ap[:, di, h:h + 1]),
                    [bks[h // 4], KDEC], [ktl])
            return ktl

        def state_update(ktl, vts, di, sbf):
            for h in range(8):
                for dkt in range(2):
                    bk = self.bank()
                    mm(bk.ap, ktl.ap[:, h, dkt * 128:(dkt + 1) * 128], vts.ap[:, h * 512:(h + 1) * 512], True, True,
                       [ktl, vts], [bk])
                    s_ = S32[h * 2 + dkt]
                    dve(lambda e, s_=s_, bk=bk, h=h: e.scalar_tensor_tensor(
                        out=s_.ap, in0=s_.ap, scalar=CD.ap[:, di, h:h + 1], in1=bk.ap, op0=ALU.mult, op1=ALU.add),
                        [s_, CD, bk], [s_])
                    if sbf is not None:
                        pool(lambda e, s_=s_, h=h, dkt=dkt: e.tensor_copy(out=sbf.ap[:, h * 2 + dkt, :], in_=s_.ap),
                             [s_, sbf], [sbf])

        def zero_state():
            for s_ in S32:
                dve(lambda e, s_=s_: e.memset(s_.ap, 0.0), [], [s_])

        S32, KTs, VTs, KTL = sweep_bufs()
        SBF_all = [self.alloc([16, 512], BF16, f"sbf{i}", ds=True) for i in range(2)]
        zero_state()
        seqB = [2 * NC + 1, 2 * NC] + list(range(2 * NC - 1, NC - 1, -1)) + list(range(NC - 1, 0, -1))
        for idx, c in enumerate(seqB):
            kts, vts = load_kv(c)
            ktl = k_tilde(kts, 1)
            snap = 1 <= c <= NC
            sbf = None
            if snap:
                sbf = SBF_all[cntS["sbf"] % 2]
                cntS["sbf"] += 1
            state_update(ktl, vts, 1, sbf)
            if snap:
                dma("sp", SBS[c - 1], sbf.ap, [sbf], [sbsT[c - 1]], sbf.ds)

        self.phase_begin()
        QD = [self.alloc([16, 128], F32, f"qd{i}") for i in range(2)]
        MK = self.alloc([8, 128], F32, "mask")
        tmpm = self.alloc([128], F32, "tmpm")
        for h in range(8):
            for di, src in ((0, fi1), (1, fi2)):
                for j in range(2):
                    act(lambda e, h=h, di=di, src=src, j=j: e.activation(
                        out=QD[di].ap[:, 2 * h + j, :], in_=src, func=AF.Exp, scale=LG.ap[:, di, h:h + 1]),
                        [CT, LG, QD[di]], [QD[di]])
            act(lambda e, h=h: e.activation(out=MK.ap[:, h, :], in_=relp, func=AF.Exp, scale=LG.ap[:, 0, h:h + 1]),
                [CT, LG, MK], [MK])
            dve(lambda e, h=h: e.tensor_tensor(out=MK.ap[:, h, :], in0=MK.ap[:, h, :], in1=mge, op=ALU.mult),
                [MK, CT], [MK])
            act(lambda e, h=h: e.activation(out=tmpm.ap, in_=reln, func=AF.Exp, scale=LG.ap[:, 1, h:h + 1]),
                [CT, LG], [tmpm])
            dve(lambda e: e.tensor_tensor(out=tmpm.ap, in0=tmpm.ap, in1=mle, op=ALU.mult), [tmpm, CT], [tmpm])
            dve(lambda e, h=h: e.tensor_tensor(out=MK.ap[:, h, :], in0=MK.ap[:, h, :], in1=tmpm.ap, op=ALU.add),
                [MK, tmpm], [MK])
        dve(lambda e: e.tensor_scalar(out=MK.ap, in0=MK.ap, scalar1=1.0 / 16.0, scalar2=None, op0=ALU.mult),
            [MK], [MK])
        S32, KTs, VTs, KTL = sweep_bufs()
        sbf_f = self.alloc([16, 512], BF16, "sbf_f")
        zero_state()
        QTs = [self.alloc([16, 128], BF16, f"qts{i}", ds=True) for i in range(2)]
        GTs = [self.alloc([32, 128], BF16, f"gts{i}", ds=True) for i in range(2)]
        SBcs = [self.alloc([16, 512], BF16, f"sbc{i}", ds=True) for i in range(2)]
        QFs = [self.alloc([16, 128], BF16, f"qf{i}") for i in range(2)]
        QBs = [self.alloc([16, 128], BF16, f"qb{i}") for i in range(2)]
        STs = [self.alloc([8, 128], BF16, f"st{i}") for i in range(2)]
        ONs = [self.alloc([512], BF16, f"on{i}") for i in range(4)]
        GONs = [self.alloc([32, 128], BF16, f"gon{i}", ds=True) for i in range(2)]
        STAT = [self.alloc([32], F32, f"stat{i}") for i in range(2)]
        JUNK = self.alloc([512], F32, "junk")
        for ci, c in enumerate([2 * NC, 2 * NC + 1]):
            kts, vts = load_kv(c)
            ktl = k_tilde(kts, 0)
            state_update(ktl, vts, 0, sbf_f if ci == 1 else None)
        for c in range(NC):
            kts, vts = load_kv(c)
            qts, gts, SBc, QF, QB = QTs[c % 2], GTs[c % 2], SBcs[c % 2], QFs[c % 2], QBs[c % 2]
            dma("sp", qts.ap, qtT[c].ap, [qtT[c]], [qts], qts.ds)
            dma("sp", gts.ap, gtT[c].ap, [gtT[c]], [gts], gts.ds)
            dma("sp", SBc.ap, SBS[c], [sbsT[c]], [SBc], SBc.ds)
            pool(lambda e, qts=qts, QF=QF: e.tensor_tensor(out=QF.ap, in0=qts.ap, in1=QD[0].ap, op=ALU.mult),
                 [qts, QD[0]], [QF])
            pool(lambda e, qts=qts, QB=QB: e.tensor_tensor(out=QB.ap, in0=qts.ap, in1=QD[1].ap, op=ALU.mult),
                 [qts, QD[1]], [QB])
            stt = STs[c % 2]
            for hb in range(2):
                bk = self.bank()
                for hh in range(4):
                    h = hb * 4 + hh
                    for dkt in range(2):
                        mm(bk.ap[:, hh * 128:(hh + 1) * 128], kts.ap[:, 2 * h + dkt, :], qts.ap[:, 2 * h + dkt, :],
                           dkt == 0, dkt == 1, [kts, qts], [bk])
                dve(lambda e, bk=bk, hb=hb, stt=stt: e.tensor_tensor(
                    out=stt.ap[:, hb * 4:(hb + 1) * 4, :], in0=bk.ap.rearrange("p (h i) -> p h i", i=128),
                    in1=MK.ap[:, hb * 4:(hb + 1) * 4, :], op=ALU.mult), [bk, MK, stt], [stt])
            ktl = k_tilde(kts, 0)
            gon = GONs[c % 2]
            for hb in range(2):
                sa = STAT[hb]
                dve(lambda e, sa=sa: e.memset(sa.ap[:, 0:8], 0.0), [], [sa])
                obk = []
                for hh in range(4):
                    h = hb * 4 + hh
                    bk = self.bank()
                    obk.append(bk)
                    mm(bk.ap, stt.ap[:, h, :], vts.ap[:, h * 512:(h + 1) * 512], True, False, [stt, vts], [bk])
                    for dkt in range(2):
                        mm(bk.ap, QF.ap[:, 2 * h + dkt, :], sbf_f.ap[:, 2 * h + dkt, :], False, False,
                           [QF, sbf_f], [bk])
                    for dkt in range(2):
                        mm(bk.ap, QB.ap[:, 2 * h + dkt, :], SBc.ap[:, 2 * h + dkt, :], False, dkt == 1,
                           [QB, SBc], [bk])
                    act(lambda e, bk=bk, sa=sa, hh=hh: e.activation(out=JUNK.ap, in_=bk.ap, func=AF.Identity,
                                                                    accum_out=sa.ap[:, hh:hh + 1]),
                        [bk, sa], [JUNK, sa])
                    act(lambda e, bk=bk, sa=sa, hh=hh: e.activation(out=JUNK.ap, in_=bk.ap, func=AF.Square,
                                                                    accum_out=sa.ap[:, 4 + hh:5 + hh]),
                        [bk, sa], [JUNK, sa])
                dve(lambda e, sa=sa: e.tensor_scalar(out=sa.ap[:, 8:12], in0=sa.ap[:, 0:4], scalar1=1.0 / DV,
                                                     scalar2=None, op0=ALU.mult), [sa], [sa])
                dve(lambda e, sa=sa: e.tensor_tensor(out=sa.ap[:, 12:16], in0=sa.ap[:, 8:12], in1=sa.ap[:, 8:12],
                                                     op=ALU.mult), [sa], [sa])
                dve(lambda e, sa=sa: e.scalar_tensor_tensor(out=sa.ap[:, 16:20], in0=sa.ap[:, 4:8], scalar=1.0 / DV,
                                                            in1=sa.ap[:, 12:16], op0=ALU.mult, op1=ALU.subtract),
                    [sa], [sa])
                act(lambda e, sa=sa: e.activation(out=sa.ap[:, 20:24], in_=sa.ap[:, 16:20], func=AF.Sqrt,
                                                  bias=epsc), [sa, CC], [sa])
                dve(lambda e, sa=sa: e.reciprocal(out=sa.ap[:, 24:28], in_=sa.ap[:, 20:24]), [sa], [sa])
                dve(lambda e, sa=sa: e.scalar_tensor_tensor(out=sa.ap[:, 28:32], in0=sa.ap[:, 8:12], scalar=-1.0,
                                                            in1=sa.ap[:, 24:28], op0=ALU.mult, op1=ALU.mult),
                    [sa], [sa])
                for hh in range(4):
                    h = hb * 4 + hh
                    bk = obk[hh]
                    on = ONs[h % 4]
                    act(lambda e, bk=bk, sa=sa, on=on, hh=hh: e.activation(
                        out=on.ap, in_=bk.ap, func=AF.Identity, scale=sa.ap[:, 24 + hh:25 + hh],
                        bias=sa.ap[:, 28 + hh:29 + hh]), [bk, sa], [on])
                    bt = self.bank()
                    btv = bt.ap.bitcast(BF16)
                    for j in range(4):
                        pe(lambda e, btv=btv, j=j, on=on: e.transpose(out=btv[:, j * 128:(j + 1) * 128],
                                                                      in_=on.ap[:, j * 128:(j + 1) * 128],
                                                                      identity=identb.ap), [on, identb], [bt])
                    dve(lambda e, btv=btv, h=h, gon=gon, gts=gts: e.tensor_tensor(
                        out=gon.ap[:, h * 4:(h + 1) * 4, :], in0=btv[:, 0:512].rearrange("p (a b) -> p a b", b=128),
                        in1=gts.ap[:, h * 4:(h + 1) * 4, :], op=ALU.mult), [bt, gts, gon], [gon])
            dma("sp", GON[c], gon.ap, [gon], [gonT[c]], gon.ds)
            if c < NC - 1:
                state_update(ktl, vts, 0, sbf_f)

        self.phase_begin()
        M2 = self.alloc([D], F32, "m2bc", ds=True)
        dma("sp", M2.ap, BCS[0], [BCST[0]], [M2], M2.ds)
        GNg = self.alloc([32, G], BF16, "gong", ds=True)
        SGs = [self.alloc([G], BF16, f"sgs{i}", ds=True) for i in range(2)]
        MCs = [self.alloc([G], F32, f"mcs{i}", ds=True) for i in range(2)]
        TMP = [self.alloc([G], F32, f"tmp{i}") for i in range(2)]
        MT = self.alloc([16, G], BF16, "mT")
        RRO = [self.alloc([32, 128], BF16, f"rro{i}", ds=True) for i in range(3)]
        RWO = [self.alloc([16, 512], BF16, f"rwo{i}", ds=True) for i in range(2)]
        XTs = [self.alloc([D], F32, f"x1t{i}", ds=True) for i in range(4)]
        XN2 = self.alloc([D], F32, "xn2")
        SS2 = [self.alloc([4], F32, f"ss2_{i}") for i in range(2)]
        A2 = self.alloc([16, G], BF16, "a2", ds=True)
        x1T = [T(X1[t * 128:(t + 1) * 128, :], f"x1_{t}") for t in range(NC)]
        a2T = [T(A2T[g], f"a2t{g}") for g in range(NG)]
        ro_units = []
        wo_units = []
        for g in range(NG):
            ro_units += [(wroT[f], lambda sap: sap) for f in range(16)]
            wo_units += [(woT[c], lambda sap: sap) for c in range(4)]
        st_ro = self.Stream(self, ro_units, RRO)
        st_wo = self.Stream(self, wo_units, RWO)
        for g in range(NG):
            for cc in range(4):
                c = g * 4 + cc
                dma("sp", GNg.ap[:, :, cc * 128:(cc + 1) * 128], GON[c], [gonT[c]], [GNg], GNg.ds,
                    append=(cc > 0))
            for tt in range(4):
                xt = XTs[tt]
                r0 = g * G + tt * 128
                dma("sp", xt.ap, xall[r0:r0 + 128, :], [], [xt], xt.ds)
            for ft in range(16):
                slot = st_ro.get(g * 16 + ft)
                sgs, mcs, tmp = SGs[ft % 2], MCs[ft % 2], TMP[ft % 2]
                dma("sp", sgs.ap, sgrT[ft][g].ap, [sgrT[ft][g]], [sgs], sgs.ds)
                dma("sp", mcs.ap, mcsT[ft][g].ap, [mcsT[ft][g]], [mcs], mcs.ds)
                bk = self.bank()
                for kc in range(32):
                    mm(bk.ap, slot.ap[:, kc, :], GNg.ap[:, kc, :], kc == 0, kc == 31, [slot, GNg], [bk])
                dve(lambda e, bk=bk, sgs=sgs, tmp=tmp: e.tensor_tensor(out=tmp.ap, in0=bk.ap, in1=sgs.ap, op=ALU.mult),
                    [bk, sgs], [tmp])
                dve(lambda e, tmp=tmp, mcs=mcs, ft=ft: e.tensor_tensor(out=MT.ap[:, ft, :], in0=tmp.ap, in1=mcs.ap,
                                                                       op=ALU.add), [tmp, mcs, MT], [MT])
            for cb in range(4):
                slot = st_wo.get(g * 4 + cb)
                for tt in range(4):
                    bk = self.bank()
                    for kc in range(16):
                        mm(bk.ap, MT.ap[:, kc, tt * 128:(tt + 1) * 128], slot.ap[:, kc, :], kc == 0, kc == 15,
                           [slot, MT], [bk])
                    xt = XTs[tt]
                    tmp = TMP[tt % 2]
                    dve(lambda e, bk=bk, tmp=tmp, cb=cb: e.tensor_tensor(
                        out=tmp.ap, in0=bk.ap, in1=M2.ap[:, cb * 512:(cb + 1) * 512], op=ALU.mult), [bk, M2], [tmp])
                    dve(lambda e, xt=xt, tmp=tmp, cb=cb: e.tensor_tensor(
                        out=xt.ap[:, cb * 512:(cb + 1) * 512], in0=xt.ap[:, cb * 512:(cb + 1) * 512], in1=tmp.ap,
                        op=ALU.add), [xt, tmp], [xt])
            for tt in range(4):
                xt = XTs[tt]
                ss = SS2[tt % 2]
                c = g * 4 + tt
                dma("sp", x1T[c].ap, xt.ap, [xt], [x1T[c]], xt.ds)
                dve(lambda e, ss=ss: e.memset(ss.ap[:, 0:1], 0.0), [], [ss])
                act(lambda e, xt=xt, ss=ss: e.activation(out=XN2.ap, in_=xt.ap, func=AF.Square,
                                                         accum_out=ss.ap[:, 0:1]), [xt], [XN2, ss])
                act(lambda e, ss=ss: e.activation(out=ss.ap[:, 1:2], in_=ss.ap[:, 0:1], func=AF.Sqrt,
                                                  scale=1.0 / D, bias=epsc), [ss, CC], [ss])
                dve(lambda e, ss=ss: e.reciprocal(out=ss.ap[:, 2:3], in_=ss.ap[:, 1:2]), [ss], [ss])
                act(lambda e, xt=xt, ss=ss: e.activation(out=XN2.ap, in_=xt.ap, func=AF.Copy,
                                                         scale=ss.ap[:, 2:3]), [xt, ss], [XN2])
                for q4 in range(4):
                    bk = self.bank()
                    for j in range(4):
                        dt_ = q4 * 4 + j
                        pe(lambda e, bk=bk, j=j, dt_=dt_: e.transpose(
                            out=bk.ap[:, j * 128:(j + 1) * 128], in_=XN2.ap[:, dt_ * 128:(dt_ + 1) * 128],
                            identity=ident), [XN2, CT], [bk])
                    for j in range(4):
                        dt_ = q4 * 4 + j
                        act(lambda e, bk=bk, j=j, dt_=dt_, tt=tt: e.activation(
                            out=A2.ap[:, dt_, tt * 128:(tt + 1) * 128], in_=bk.ap[:, j * 128:(j + 1) * 128],
                            func=AF.Identity, scale=GS.ap[:, 4, dt_:dt_ + 1], bias=GS.ap[:, 5, dt_:dt_ + 1]),
                            [bk, GS, A2], [A2])
            dma("sp", A2T[g], A2.ap, [A2], [a2T[g]], A2.ds)

        self.phase_begin()
        M5 = self.alloc([D], F32, "m5bc", ds=True)
        FGB = self.alloc([D], F32, "fgbc", ds=True)
        dma("sp", M5.ap, BCS[1], [BCST[1]], [M5], M5.ds)
        dma("sp", FGB.ap, fg.partition_broadcast(128), [], [FGB], FGB.ds)
        A2s = self.alloc([16, G], BF16, "a2s", ds=True)
        HT = self.alloc([64, G], BF16, "hT")
        X2 = [self.alloc([D], F32, f"x2_{i}", ds=True) for i in range(4)]
        X1s = [self.alloc([512], F32, f"x1s{i}", ds=True) for i in range(2)]
        R1 = [self.alloc([16, 256], BF16, f"rf1_{i}", ds=True) for i in range(3)]
        R2 = [self.alloc([8, 512], BF16, f"rf2_{i}", ds=True) for i in range(3)]
        RL = [self.alloc([G], F32, f"rl{i}") for i in range(2)]
        SS3 = [self.alloc([4], F32, f"ss3_{i}") for i in range(2)]
        JK = self.alloc([D], F32, "jk")
        f1_units = []
        f2_units = []
        for g in range(NG):
            f1_units += [(wf1T[u], lambda sap: sap) for u in range(32)]
            for c in range(4):
                f2_units += [(wf2T[c][s], lambda sap: sap) for s in range(8)]
        st_f1 = self.Stream(self, f1_units, R1)
        st_f2 = self.Stream(self, f2_units, R2)
        outT = [T(out[t * 128:(t + 1) * 128, :], f"out{t}") for t in range(NC)]
        c1 = 0
        for g in range(NG):
            dma("sp", A2s.ap, A2T[g], [a2T[g]], [A2s], A2s.ds)
            for u in range(32):
                slot = st_f1.get(g * 32 + u)
                for j in range(2):
                    ht = u * 2 + j
                    bk = self.bank()
                    for kc in range(16):
                        mm(bk.ap, slot.ap[:, kc, j * 128:(j + 1) * 128], A2s.ap[:, kc, :], kc == 0, kc == 15,
                           [slot, A2s], [bk])
                    rl = RL[ht % 2]
                    act(lambda e, bk=bk, rl=rl: e.activation(out=rl.ap, in_=bk.ap, func=AF.Relu), [bk], [rl])
                    dve(lambda e, rl=rl, ht=ht: e.tensor_tensor(out=HT.ap[:, ht, :], in0=rl.ap, in1=rl.ap, op=ALU.mult),
                        [rl, HT], [HT])
            for cb in range(4):
                bks = [self.bank() for _ in range(4)]
                for s in range(8):
                    slot = st_f2.get((g * 4 + cb) * 8 + s)
                    for tt in range(4):
                        for kc in range(8):
                            mm(bks[tt].ap, HT.ap[:, s * 8 + kc, tt * 128:(tt + 1) * 128], slot.ap[:, kc, :],
                               s == 0 and kc == 0, s == 7 and kc == 7, [slot, HT], [bks[tt]])
                for tt in range(4):
                    c = g * 4 + tt
                    x1s = X1s[c1 % 2]
                    c1 += 1
                    dma("sp", x1s.ap, X1[c * 128:(c + 1) * 128, cb * 512:(cb + 1) * 512], [x1T[c]], [x1s], x1s.ds)
                    x2 = X2[tt]
                    dve(lambda e, bk=bks[tt], x2=x2, cb=cb: e.tensor_tensor(
                        out=x2.ap[:, cb * 512:(cb + 1) * 512], in0=bk.ap, in1=M5.ap[:, cb * 512:(cb + 1) * 512],
                        op=ALU.mult), [bks[tt], M5, x2], [x2])
                    dve(lambda e, x2=x2, x1s=x1s, cb=cb: e.tensor_tensor(
                        out=x2.ap[:, cb * 512:(cb + 1) * 512], in0=x2.ap[:, cb * 512:(cb + 1) * 512], in1=x1s.ap,
                        op=ALU.add), [x2, x1s], [x2])
            for tt in range(4):
                c = g * 4 + tt
                x2 = X2[tt]
                ss = SS3[tt % 2]
                dve(lambda e, ss=ss: e.memset(ss.ap[:, 0:1], 0.0), [], [ss])
                act(lambda e, x2=x2, ss=ss: e.activation(out=JK.ap, in_=x2.ap, func=AF.Square,
                                                         accum_out=ss.ap[:, 0:1]), [x2], [JK, ss])
                act(lambda e, ss=ss: e.activation(out=ss.ap[:, 1:2], in_=ss.ap[:, 0:1], func=AF.Sqrt,
                                                  scale=1.0 / D, bias=epsc), [ss, CC], [ss])
                dve(lambda e, ss=ss: e.reciprocal(out=ss.ap[:, 2:3], in_=ss.ap[:, 1:2]), [ss], [ss])
                dve(lambda e, x2=x2, ss=ss: e.scalar_tensor_tensor(out=x2.ap, in0=x2.ap, scalar=ss.ap[:, 2:3],
                                                                   in1=FGB.ap, op0=ALU.mult, op1=ALU.mult),
                    [x2, ss, FGB], [x2])
                dma("sp", out[c * 128:(c + 1) * 128, :], x2.ap, [x2], [outT[c]], x2.ds)
        self.R.barrier()
        self.R.emit()
        return nc


def _consts():
    p = np.arange(128, dtype=np.float32)[:, None]
    i = np.arange(128, dtype=np.float32)[None, :]
    rel = i - p
    ident = (rel == 0).astype(np.float32)
    relp = np.maximum(rel, 0.0)
    reln = np.maximum(-rel, 0.0)
    mge = (rel >= 0).astype(np.float32)
    mle = (rel <= 0).astype(np.float32)
    fi1 = np.broadcast_to(i + 1.0, (128, 128))
    fi2 = np.broadcast_to(128.0 - i, (128, 128))
    pF = 127.0 - p
    pB = p
    cst = np.concatenate([ident, relp, reln, mge, mle, fi1, fi2, pF, pB], axis=1).astype(np.float32)
    half = 128
    invf = (1.0 / (np.float32(10000.0) ** np.linspace(0.0, 1.0, half, dtype=np.float32))).astype(np.float32)
    return np.ascontiguousarray(cst), invf


def make_in_maps(n_own, x, c, ctx, c_ctx, w_mod, b_mod, norm1_g, w_in, conv_w, w_conv_out,
                 ret_decay_fwd, ret_decay_bwd, w_ret_out, w_o, norm2_g, w_ff1, w_ff2, final_g):
    f = lambda a: np.ascontiguousarray(np.asarray(a, dtype=np.float32))
    x, c, ctx, c_ctx = f(x), f(c), f(ctx), f(c_ctx)
    Bn, S, _ = x.shape
    assert S == 2 * n_own
    cst, invf = _consts()
    shared = dict(cctx=f(c_ctx), w_mod=f(w_mod[0]), b_mod=f(b_mod[0]), g1=f(norm1_g[0]), g2=f(norm2_g[0]),
                  fg=f(final_g), w_in=f(w_in[0]), w_co=f(w_conv_out[0]), w_ro=f(w_ret_out[0]), w_o=f(w_o[0]),
                  w_f1=f(w_ff1[0]), w_f2=f(w_ff2[0]), cst=cst, invf=invf)
    cw = f(conv_w[0])
    dF, dB = f(ret_decay_fwd[0]), f(ret_decay_bwd[0])
    maps = []
    for b in range(Bn):
        for half in range(2):
            if half == 0:
                xl = x[b]
                cl = ctx[b]
                posl = CTX + np.arange(S, dtype=np.float32)
                posc = np.arange(CTX, dtype=np.float32)
                m = dict(decF=dF, decB=dB, conv_w=cw)
            else:
                xl = x[b, ::-1]
                cl = ctx[b, ::-1]
                posl = CTX + np.arange(S, dtype=np.float32)[::-1]
                posc = np.arange(CTX, dtype=np.float32)[::-1]
                m = dict(decF=dB, decB=dF, conv_w=np.ascontiguousarray(cw[::-1]))
            m["xall"] = np.ascontiguousarray(np.concatenate([xl, cl], axis=0))
            m["pos"] = np.ascontiguousarray(np.concatenate([posl, posc]).astype(np.float32))
            m["cvec"] = np.ascontiguousarray(c[b])
            m.update(shared)
            maps.append(m)
    return maps


_NC_CACHE = {}


def run(n_own, inputs, debug=False, trace=False):
    key = (n_own, debug)
    if key not in _NC_CACHE:
        _NC_CACHE[key] = Builder(n_own, debug=debug).build()
    nc = _NC_CACHE[key]
    maps = make_in_maps(n_own, **inputs)
    res = run_bass_kernel_spmd(nc, maps, core_ids=list(range(len(maps))), trace=trace)
    x = np.asarray(inputs["x"])
    Bn, S, _ = x.shape
    outp = np.empty((Bn, S, D), dtype=np.float32)
    for b in range(Bn):
        for half in range(2):
            o = res.results[b * 2 + half]["out"]
            if half == 0:
                outp[b, :n_own] = o
            else:
                outp[b, n_own:] = o[::-1]
    return outp, res


def kernel(**inputs):
    outp, _ = run(4096, inputs)
    return outp
```

```python
import math
from contextlib import ExitStack

import numpy as np
import ml_dtypes

import concourse.bass as bass
import concourse.mybir as mybir
from concourse.bass_utils import run_bass_kernel_spmd

F32 = mybir.dt.float32
BF16 = mybir.dt.bfloat16
I32 = mybir.dt.int32
AF = mybir.ActivationFunctionType
ALU = mybir.AluOpType

D = 2048
NH = 8
DK = 256
DV = 512
DFF = 8192
CTX = 256
EPS = 1e-6
NMOD = 6
G = 512
OFF = {"conv_b": 0, "conv_c": 2048, "conv_x": 4096, "q": 6144, "k": 8192, "v": 10240,
       "g": 14336, "gate_conv": 18432, "gate_ret": 20480}
D_IN = 22528
TWO_PI_HI = 6.28125
TWO_PI_LO = 2.0 * math.pi - 6.28125


class Tok:
    __slots__ = ("eng", "sem", "val", "need", "dma")


class Buf:
    __slots__ = ("name", "w", "rs")

    def __init__(self, name=""):
        self.name = name
        self.w = []
        self.rs = {}


class DSem:
    __slots__ = ("sem", "count")

    def __init__(self, sem):
        self.sem = sem
        self.count = 0


class T:
    __slots__ = ("ap", "buf", "ds")

    def __init__(self, ap, name="", ds=None):
        self.ap = ap
        self.buf = Buf(name)
        self.ds = ds


ENGS = ("pe", "act", "dve", "pool", "sp")


class Rec:
    def __init__(self, nc, stack):
        self.nc = nc
        self.stack = stack
        self.ops = {e: [] for e in ENGS}
        self.pending = {e: [] for e in ENGS}
        self.last = {e: None for e in ENGS}
        self.dmas = {}
        self.nsem = 0

    def new_sem(self, name):
        self.nsem += 1
        return self.stack.enter_context(self.nc.semaphore(f"{name}_{self.nsem}"))

    def dsem(self, name="d"):
        return DSem(self.new_sem(name))

    def op(self, eng, fn, reads=(), writes=(), dsem=None, append=False, nobarrier=False):
        t = Tok()
        t.eng = eng
        t.dma = dsem is not None
        t.need = False
        t.sem = None
        t.val = None
        if t.dma:
            dsem.count += 16
            t.sem = dsem.sem
            t.val = dsem.count
            t.need = True
        waits = self.pending[eng]
        self.pending[eng] = []

        def dep(d, raw):
            if d.dma:
                waits.append(d)
                return
            if d.eng == eng and not t.dma:
                if eng == "pe" or not raw:
                    return
            d.need = True
            waits.append(d)

        for b in reads:
            for d in b.w:
                dep(d, True)
        if not append:
            for b in writes:
                for d in b.w:
                    dep(d, False)
                for d in b.rs.values():
                    dep(d, False)
        key = ("d", id(dsem)) if t.dma else eng
        for b in reads:
            b.rs[key] = t
        for b in writes:
            if append:
                b.w.append(t)
            else:
                b.w = [t]
                b.rs = {}
        self.ops[eng].append((fn, waits, t))
        if t.dma:
            if not nobarrier:
                self.dmas[id(dsem)] = t
        else:
            self.last[eng] = t
        return t

    def barrier(self):
        toks = [t for t in self.last.values() if t is not None] + list(self.dmas.values())
        for t in toks:
            if not t.dma:
                t.need = True
        for e in ENGS:
            self.pending[e] = self.pending[e] + toks
        self.dmas = {}

    def emit(self):
        nc = self.nc
        for e in ("pe", "act", "dve", "pool"):
            cnt = 0
            sem = None
            for fn, waits, t in self.ops[e]:
                if t.dma or not t.need:
                    continue
                if sem is None or cnt >= 30000:
                    sem = self.new_sem("c" + e)
                    cnt = 0
                cnt += 1
                t.sem = sem
                t.val = cnt

        def run(name, eng):
            waited = {}

            def do_waits(ws):
                for d in ws:
                    k = id(d.sem)
                    if waited.get(k, -1) >= d.val:
                        continue
                    eng.wait_ge(d.sem, d.val)
                    waited[k] = d.val

            for fn, waits, t in self.ops[name]:
                do_waits(waits)
                ins = fn(eng)
                if t.need:
                    ins.then_inc(t.sem, 16 if t.dma else 1)
            do_waits(self.pending[name])

        with nc.Block() as block:
            @block.tensor
            def _(eng):
                run("pe", eng)

            @block.scalar
            def _(eng):
                run("act", eng)

            @block.vector
            def _(eng):
                run("dve", eng)

            @block.gpsimd
            def _(eng):
                run("pool", eng)

            @block.sync
            def _(eng):
                run("sp", eng)


class Builder:
    def __init__(self, n_own, debug=False):
        self.N = n_own
        self.NC = n_own // 128
        self.NG = n_own // G
        self.TL = 2 * n_own + CTX
        self.debug = debug
        self.nc = bass.Bass("TRN2", target_bir_lowering=False)
        self.stack = ExitStack()
        self.R = Rec(self.nc, self.stack)
        self.bank_i = 0
        self.dspool = []
        self.dsidx = 0
        self.dsbase = 0

    def dram_in(self, name, shape, dt=F32):
        return self.nc.dram_tensor(name, list(shape), dt, kind="ExternalInput").ap()

    def dram_scratch(self, name, shape, dt, dbg=False):
        kind = "ExternalOutput" if (dbg and self.debug) else "Internal"
        return self.nc.dram_tensor(name, list(shape), dt, kind=kind).ap()

    def alloc(self, free_shape, dt, name="", ds=False):
        n = 1
        for s in free_shape:
            n *= s
        n16 = n * (2 if dt in (F32, I32) else 1)
        n16 = (n16 + 15) // 16 * 16
        assert self.aoff + n16 <= self.acap, f"SBUF arena overflow {name} {self.aoff}+{n16}>{self.acap}"
        ap = self.arena[:, self.aoff:self.aoff + n16]
        self.aoff += n16
        if dt in (F32, I32):
            ap = ap.bitcast(dt)
        ap = ap[:, 0:n]
        if len(free_shape) == 2:
            ap = ap.rearrange("p (a b) -> p a b", b=free_shape[1])
        elif len(free_shape) == 3:
            ap = ap.rearrange("p (a b c) -> p a b c", b=free_shape[1], c=free_shape[2])
        return T(ap, name, self.getds() if ds else None)

    def getds(self):
        if self.dsidx < len(self.dspool):
            d = self.dspool[self.dsidx]
        else:
            d = self.R.dsem("sb")
            self.dspool.append(d)
        self.dsidx += 1
        return d

    def bank(self):
        i = self.bank_i % 8
        self.bank_i += 1
        return self.PS[i]

    def dma(self, q, out_ap, in_ap, reads, writes, ds, append=False, nobarrier=False):
        return self.R.op(q, lambda e: e.dma_start(out=out_ap, in_=in_ap),
                         [r.buf for r in reads], [w.buf for w in writes], dsem=ds, append=append,
                         nobarrier=nobarrier)

    def pe(self, fn, reads, writes):
        return self.R.op("pe", fn, [r.buf for r in reads], [w.buf for w in writes])

    def act(self, fn, reads, writes):
        return self.R.op("act", fn, [r.buf for r in reads], [w.buf for w in writes])

    def dve(self, fn, reads, writes):
        return self.R.op("dve", fn, [r.buf for r in reads], [w.buf for w in writes])

    def mm(self, out_ap, lhsT, rhs, start, stop, reads, writes):
        return self.pe(lambda e: e.matmul(out_ap, lhsT=lhsT, rhs=rhs, start=start, stop=stop), reads, writes)

    def phase_begin(self):
        self.R.barrier()
        self.aoff = self.abase
        self.dsidx = self.dsbase

    class Stream:
        def __init__(self, B, units, slots, pre=None):
            self.B = B
            self.units = units
            self.slots = slots
            self.issued = 0
            self.pre = pre if pre is not None else len(slots) - 1

        def _issue(self):
            j = self.issued
            if j >= len(self.units):
                return
            dT, sl = self.units[j]
            slot = self.slots[j % len(self.slots)]
            self.B.dma("sp", sl(slot.ap), sl(dT.ap), [dT], [slot], slot.ds)
            self.issued += 1

        def get(self, j):
            while self.issued <= min(j + self.pre, len(self.units) - 1):
                self._issue()
            return self.slots[j % len(self.slots)]

    def build(self):
        nc, R, N, NC, NG, TL = self.nc, self.R, self.N, self.NC, self.NG, self.TL
        st = self.stack
        xall = self.dram_in("xall", [TL, D])
        pos = self.dram_in("pos", [TL])
        cvec = self.dram_in("cvec", [D])
        cctx = self.dram_in("cctx", [D])
        w_mod = self.dram_in("w_mod", [D, NMOD * D])
        b_mod = self.dram_in("b_mod", [NMOD * D])
        g1 = self.dram_in("g1", [D])
        g2 = self.dram_in("g2", [D])
        fg = self.dram_in("fg", [D])
        w_in = self.dram_in("w_in", [D, D_IN])
        conv_w = self.dram_in("conv_w", [3, D])
        w_co = self.dram_in("w_co", [D, D])
        decF = self.dram_in("decF", [NH])
        decB = self.dram_in("decB", [NH])
        w_ro = self.dram_in("w_ro", [NH * DV, D])
        w_o = self.dram_in("w_o", [D, D])
        w_f1 = self.dram_in("w_f1", [D, DFF])
        w_f2 = self.dram_in("w_f2", [DFF, D])
        cst = self.dram_in("cst", [128, 7 * 128 + 2])
        invf = self.dram_in("invf", [128])
        out = self.nc.dram_tensor("out", [N, D], F32, kind="ExternalOutput").ap()
        NCH = 2 * NC + 2
        WIN = self.dram_scratch("WIN", [52, 128, 16, 512], BF16)
        WRO = self.dram_scratch("WRO", [16, 128, 32, 128], BF16)
        WO = self.dram_scratch("WO", [4, 128, 16, 512], BF16)
        WF1 = self.dram_scratch("WF1", [32, 128, 16, 256], BF16)
        WF2 = self.dram_scratch("WF2", [4, 8, 128, 8, 512], BF16)
        QT = self.dram_scratch("QT", [NC, 128, 16, 128], BF16, dbg=True)
        KT = self.dram_scratch("KT", [NCH, 128, 16, 128], BF16, dbg=True)
        VS = self.dram_scratch("VS", [TL, NH * DV], BF16, dbg=True)
        GT = self.dram_scratch("GT", [NC, 128, 32, 128], BF16, dbg=True)
        SGR = self.dram_scratch("SGR", [16, 128, N], BF16, dbg=True)
        MCS = self.dram_scratch("MCS", [16, 128, N], F32, dbg=True)
        SBS = self.dram_scratch("SBS", [NC, 128, 16, 512], BF16, dbg=True)
        GON = self.dram_scratch("GON", [NC, 128, 32, 128], BF16, dbg=True)
        X1 = self.dram_scratch("X1", [N, D], F32, dbg=True)
        A2T = self.dram_scratch("A2T", [NG, 128, 16, G], BF16, dbg=True)
        BCS = self.dram_scratch("BCS", [2, 128, D], F32)
        self.acap = 212000 // 2 // 16 * 16
        self.arena = st.enter_context(nc.sbuf_tensor("arena", [128, self.acap], BF16))
        self.aoff = 0
        self.PS = []
        for i in range(8):
            p = st.enter_context(nc.psum_tensor(f"ps{i}", [128, 512], F32))
            self.PS.append(T(p[:, :], f"ps{i}"))

        dma, pe, act, dve, mm = self.dma, self.pe, self.act, self.dve, self.mm
        B = self

        cvt = []
        winT = [T(WIN[u], f"win{u}") for u in range(52)]
        conv_order = list(range(28, 40)) + list(range(0, 28)) + list(range(40, 52))
        grp_of = {}
        for i_, u_ in enumerate(conv_order):
            grp_of[u_] = (i_ // 4) if i_ < 12 else 3 + (i_ - 12) // 8
        win_grp_ds = [R.dsem(f"wing{i}") for i in range(8)]
        win_ds = [win_grp_ds[grp_of[u]] for u in range(52)]

        def conv_cols(u, src_ap_cols, c0, ncols, K=D, kcn=16, dstT=None, dst=None, ds=None):
            for q in range(0, kcn, 4):
                src = src_ap_cols[q * 128:(q + 4) * 128, :].rearrange("(kc p) c -> p kc c", p=128)
                cvt.append(dma("pool", dst[:, q:q + 4, c0:c0 + ncols], src, [], [dstT], ds, append=True,
                               nobarrier=True))

        def win_unit_cols(u):
            if u < 16:
                ct = u
                return [(0, 128, w_in[:, OFF["conv_c"] + ct * 128: OFF["conv_c"] + (ct + 1) * 128]),
                        (128, 128, w_in[:, OFF["conv_x"] + ct * 128: OFF["conv_x"] + (ct + 1) * 128]),
                        (256, 128, w_in[:, OFF["conv_b"] + ct * 128: OFF["conv_b"] + (ct + 1) * 128])]
            if u < 20:
                o = OFF["gate_conv"] + (u - 16) * 512
            elif u < 24:
                return [(0, 512, w_co[:, (u - 20) * 512:(u - 19) * 512])]
            elif u < 28:
                o = OFF["q"] + (u - 24) * 512
            elif u < 32:
                o = OFF["k"] + (u - 28) * 512
            elif u < 40:
                o = OFF["v"] + (u - 32) * 512
            elif u < 48:
                o = OFF["g"] + (u - 40) * 512
            else:
                o = OFF["gate_ret"] + (u - 48) * 512
            return [(0, 512, w_in[:, o:o + 512])]

        for u in conv_order:
            for (c0, ncols, src) in win_unit_cols(u):
                conv_cols(u, src, c0, ncols, dstT=winT[u], dst=WIN[u], ds=win_ds[u])
        wroT = [T(WRO[f], f"wro{f}") for f in range(16)]
        ds_ro = R.dsem("wro")
        for f in range(16):
            for q in range(0, 32, 4):
                src = w_ro[q * 128:(q + 4) * 128, f * 128:(f + 1) * 128].rearrange("(kc p) c -> p kc c", p=128)
                cvt.append(dma("pool", WRO[f][:, q:q + 4, :], src, [], [wroT[f]], ds_ro, append=True,
                               nobarrier=True))
        woT = [T(WO[c], f"wo{c}") for c in range(4)]
        ds_wo = R.dsem("wo")
        for c in range(4):
            conv_cols(0, w_o[:, c * 512:(c + 1) * 512], 0, 512, dstT=woT[c], dst=WO[c], ds=ds_wo)
        wf1T = [T(WF1[u], f"wf1{u}") for u in range(32)]
        ds_f1 = R.dsem("wf1")
        for u in range(32):
            conv_cols(0, w_f1[:, u * 256:(u + 1) * 256], 0, 256, dstT=wf1T[u], dst=WF1[u], ds=ds_f1)
        wf2T = [[T(WF2[c, s], f"wf2{c}_{s}") for s in range(8)] for c in range(4)]
        ds_f2 = R.dsem("wf2")
        for c in range(4):
            for s in range(8):
                for q in range(0, 8, 4):
                    r0 = (s * 8 + q) * 128
                    src = w_f2[r0:r0 + 512, c * 512:(c + 1) * 512].rearrange("(kc p) c -> p kc c", p=128)
                    cvt.append(dma("pool", WF2[c, s][:, q:q + 4, :], src, [], [wf2T[c][s]], ds_f2, append=True,
                                   nobarrier=True))
        for t_ in cvt:
            for ds_ in [ds_ro, ds_wo, ds_f1, ds_f2] + win_grp_ds:
                if t_.sem is ds_.sem:
                    t_.val = ds_.count

        CT = self.alloc([7 * 128 + 2], F32, "CT", ds=True)
        dma("sp", CT.ap, cst, [], [CT], CT.ds)
        ident = CT.ap[:, 0:128]
        relp = CT.ap[:, 128:256]
        reln = CT.ap[:, 256:384]
        mge = CT.ap[:, 384:512]
        mle = CT.ap[:, 512:640]
        fi1 = CT.ap[:, 640:768]
        fi2 = CT.ap[:, 768:896]
        pF = CT.ap[:, 896:897]
        pB = CT.ap[:, 897:898]
        identb = self.alloc([128], BF16, "identb")
        dve(lambda e: e.tensor_copy(out=identb.ap, in_=ident), [CT], [identb])
        invfT = self.alloc([1], F32, "invf", ds=True)
        dma("sp", invfT.ap, invf.rearrange("(p o) -> p o", o=1), [], [invfT], invfT.ds)
        CC = self.alloc([4], F32, "CC")
        dve(lambda e: e.memset(CC.ap[:, 0:1], EPS), [], [CC])
        dve(lambda e: e.memset(CC.ap[:, 1:2], math.pi / 2), [CC], [CC])
        epsc = CC.ap[:, 0:1]
        hpic = CC.ap[:, 1:2]
        VC = self.alloc([128], F32, "VC")
        BC = self.alloc([96], F32, "BC")
        MODC = self.alloc([4, 16, 2], F32, "MODC")
        GS = self.alloc([6, 16], F32, "GS")
        LG = self.alloc([2, 8], F32, "LG")
        KDEC = self.alloc([2, 8], F32, "KDEC")
        CD = self.alloc([2, 8], F32, "CD")
        SC = self.alloc([16, 2], F32, "SC")
        self.abase = self.aoff
        self.dsbase = self.dsidx

        VR = self.alloc([128], F32, "VR", ds=True)
        BR = self.alloc([128], F32, "BR", ds=True)
        dve(lambda e: e.memset(VR.ap, 0.0), [], [VR])
        dve(lambda e: e.memset(BR.ap, 0.0), [], [BR])
        dma("sp", VR.ap[0:16, :], cvec.rearrange("(a b) -> a b", b=128), [], [VR], VR.ds)
        dma("sp", VR.ap[16:32, :], cctx.rearrange("(a b) -> a b", b=128), [VR], [VR], VR.ds, append=True)
        dma("sp", VR.ap[32:48, :], g1.rearrange("(a b) -> a b", b=128), [VR], [VR], VR.ds, append=True)
        dma("sp", VR.ap[48:64, :], g2.rearrange("(a b) -> a b", b=128), [VR], [VR], VR.ds, append=True)
        dma("sp", VR.ap[64:112, :], conv_w.rearrange("t (a b) -> (t a) b", b=128), [VR], [VR], VR.ds, append=True)
        dma("sp", BR.ap[0:96, :], b_mod.rearrange("(a b) -> a b", b=128), [], [BR], BR.ds)
        bk = self.bank()
        pe(lambda e: e.transpose(out=bk.ap[:, 0:128], in_=VR.ap, identity=ident), [VR, CT], [bk])
        dve(lambda e: e.tensor_copy(out=VC.ap, in_=bk.ap[:, 0:128]), [bk], [VC])
        bk2 = self.bank()
        pe(lambda e: e.transpose(out=bk2.ap[:, 0:128], in_=BR.ap, identity=ident), [BR, CT], [bk2])
        dve(lambda e: e.tensor_copy(out=BC.ap, in_=bk2.ap[:, 0:96]), [bk2], [BC])
        act(lambda e: e.activation(out=SC.ap[:, :, 0], in_=VC.ap[:, 0:16], func=AF.Silu), [VC], [SC])
        act(lambda e: e.activation(out=SC.ap[:, :, 1], in_=VC.ap[:, 16:32], func=AF.Silu), [VC, SC], [SC])
        BCST = [T(BCS[i], f"bcs{i}") for i in range(2)]
        mod_env = {}

        def mod_setup_bufs():
            SCB = self.alloc([16, 128], F32, "SCB")
            dve(lambda e: e.tensor_copy(out=SCB.ap, in_=SC.ap[:, :, 0:1].to_broadcast([128, 16, 128])), [SC], [SCB])
            mod_env["SCB"] = SCB
            mod_env["WM"] = [self.alloc([16, 512], F32, f"WM{i}", ds=True) for i in range(2)]
            mod_env["BCr"] = [self.alloc([D], F32, f"BCr{i}", ds=True) for i in range(2)]
            mod_env["bmr"] = self.alloc([512], F32, "bmr", ds=True)
            mod_env["n"] = 0

        def wm_load(m, pc):
            s_ = mod_env["WM"][mod_env["n"] % 2]
            mod_env["n"] += 1
            src = w_mod[:, m * D + pc * 512: m * D + (pc + 1) * 512].rearrange("(kc p) c -> p kc c", p=128)
            dma("sp", s_.ap, src, [], [s_], s_.ds)
            return s_

        def mod_piece(m, pc, s_):
            SCB, BCr, bmr = mod_env["SCB"], mod_env["BCr"], mod_env["bmr"]
            if m in (0, 1, 3, 4):
                mi = (0, 1, 3, 4).index(m)
                bk = self.bank()
                for dtl in range(4):
                    for kc in range(16):
                        mm(bk.ap[:, dtl * 2:dtl * 2 + 2], s_.ap[:, kc, dtl * 128:(dtl + 1) * 128], SC.ap[:, kc, :],
                           kc == 0, kc == 15, [s_, SC], [bk])
                for dtl in range(4):
                    dt_ = pc * 4 + dtl
                    dve(lambda e, bk=bk, dtl=dtl, mi=mi, dt_=dt_, m=m: e.tensor_tensor(
                        out=MODC.ap[:, mi, dt_, :], in0=bk.ap[:, dtl * 2:dtl * 2 + 2],
                        in1=BC.ap[:, m * 16 + dt_: m * 16 + dt_ + 1].to_broadcast([128, 2]), op=ALU.add),
                        [bk, BC, MODC], [MODC])
            else:
                j = 0 if m == 2 else 1
                bk = self.bank()
                for kc in range(16):
                    mm(bk.ap, SCB.ap[:, kc, :], s_.ap[:, kc, :], kc == 0, kc == 15, [s_, SCB], [bk])
                dma("sp", bmr.ap, b_mod[m * D + pc * 512: m * D + (pc + 1) * 512].partition_broadcast(128),
                    [], [bmr], bmr.ds)
                dve(lambda e, bk=bk, j=j, pc=pc: e.tensor_tensor(
                    out=BCr[j].ap[:, pc * 512:(pc + 1) * 512], in0=bk.ap, in1=bmr.ap, op=ALU.add),
                    [bk, bmr, BCr[j]], [BCr[j]])
                if pc == 3:
                    dma("sp", BCS[j], BCr[j].ap, [BCr[j]], [BCST[j]], BCr[j].ds)

        def mod_run(plist):
            slots = [wm_load(*plist[0])]
            for i, (m, pc) in enumerate(plist):
                if i + 1 < len(plist):
                    slots.append(wm_load(*plist[i + 1]))
                mod_piece(m, pc, slots[i])

        early = [(m, pc) for m in (0, 1) for pc in range(4)]
        late = [(m, pc) for m in (3, 4, 2, 5) for pc in range(4)]
        mod_setup_bufs()
        mod_run(early)
        def gs_make(dst_g, dst_s, mi_shift, mi_scale, which, gcol0):
            dve(lambda e: e.scalar_tensor_tensor(out=GS.ap[:, dst_g, :], in0=MODC.ap[:, mi_scale, :, which],
                                                 scalar=1.0, in1=VC.ap[:, gcol0:gcol0 + 16],
                                                 op0=ALU.add, op1=ALU.mult), [MODC, VC, GS], [GS])
            dve(lambda e: e.tensor_copy(out=GS.ap[:, dst_s, :], in_=MODC.ap[:, mi_shift, :, which]),
                [MODC, GS], [GS])

        gs_make(0, 1, 0, 1, 0, 32)
        gs_make(2, 3, 0, 1, 1, 32)
        dT_ = self.alloc([2, 8], F32, "decin", ds=True)
        dma("sp", dT_.ap[:, 0, :], decF.partition_broadcast(128), [], [dT_], dT_.ds)
        dma("sp", dT_.ap[:, 1, :], decB.partition_broadcast(128), [dT_], [dT_], dT_.ds, append=True)
        tmpe = self.alloc([2, 8], F32, "tmpe")
        act(lambda e: e.activation(out=tmpe.ap, in_=dT_.ap, func=AF.Exp, scale=-1.0), [dT_], [tmpe])
        act(lambda e: e.activation(out=tmpe.ap, in_=tmpe.ap, func=AF.Ln, bias=1.0), [tmpe], [tmpe])
        dve(lambda e: e.tensor_scalar(out=LG.ap, in0=tmpe.ap, scalar1=-1.0, scalar2=None, op0=ALU.mult),
            [tmpe], [LG])
        act(lambda e: e.activation(out=KDEC.ap[:, 0, :], in_=LG.ap[:, 0, :], func=AF.Exp, scale=pF), [LG, CT], [KDEC])
        act(lambda e: e.activation(out=KDEC.ap[:, 1, :], in_=LG.ap[:, 1, :], func=AF.Exp, scale=pB), [LG, CT, KDEC], [KDEC])
        dve(lambda e: e.tensor_scalar(out=KDEC.ap, in0=KDEC.ap, scalar1=1.0 / 16.0, scalar2=None, op0=ALU.mult),
            [KDEC], [KDEC])
        act(lambda e: e.activation(out=CD.ap, in_=LG.ap, func=AF.Exp, scale=128.0), [LG], [CD])

        self.phase_begin()
        XT = [self.alloc([D], F32, f"xt{i}", ds=True) for i in range(2)]
        XNs = [self.alloc([D], F32, f"xn{i}") for i in range(2)]
        SS = [self.alloc([4], F32, f"ss{i}") for i in range(2)]
        AT = [self.alloc([16, G], BF16, f"aT{i}") for i in range(2)]
        RING = [self.alloc([16, 512], BF16, f"ring{i}", ds=True) for i in range(3)]
        POSB = self.alloc([G], F32, "posb", ds=True)
        ANG = self.alloc([G], F32, "ang")
        KI = self.alloc([G], I32, "ki")
        KF = self.alloc([G], F32, "kf")
        RR = self.alloc([G], F32, "rr")
        RC = self.alloc([G], F32, "rc")
        COS = self.alloc([G], F32, "cos")
        SIN = self.alloc([G], F32, "sin")
        YBC = self.alloc([16, G], BF16, "ybc")
        CSB = [self.alloc([G], F32, f"csb{i}") for i in range(2)]
        UP = [self.alloc([G // 64, 66], F32, f"up{i}") for i in range(2)]
        YC = [self.alloc([G], F32, f"yc{i}") for i in range(2)]
        RT = [self.alloc([G], F32, f"rt{i}") for i in range(4)]
        RO = [self.alloc([G], BF16, f"ro{i}", ds=True) for i in range(4)]
        SG = self.alloc([4, G], BF16, "sg")
        MCO = [self.alloc([G], F32, f"mco{i}", ds=True) for i in range(2)]
        VO = [self.alloc([512], BF16, f"vo{i}", ds=True) for i in range(3)]
        GO = [self.alloc([G], BF16, f"go{i}", ds=True) for i in range(3)]
        for u_ in UP:
            dve(lambda e, u_=u_: e.memset(u_.ap, 0.0), [], [u_])

        groups = []
        groups.append(("kv", 2 * N, CTX, 1))
        for g in range(NG):
            groups.append(("kv", N + g * G, G, 0))
        for g in range(NG):
            groups.append(("full", g * G, G, 0))
        full_units = list(range(0, 16))
        for fq in range(4):
            full_units += [16 + fq, 20 + fq]
        full_units += list(range(24, 52))
        kv_units = list(range(28, 40))
        unit_seq = []
        for (kind, tok0, T_, ms) in groups:
            unit_seq += kv_units if kind == "kv" else full_units
        ktT = [T(KT[c], f"kt{c}") for c in range(NCH)]
        vsT = [T(VS[c * 128:(c + 1) * 128, :], f"vs{c}") for c in range(NCH)]
        qtT = [T(QT[c], f"qt{c}") for c in range(NC)]
        gtT = [T(GT[c], f"gt{c}") for c in range(NC)]
        sgrT = [[T(SGR[f, :, g * G:(g + 1) * G], f"sgr{f}_{g}") for g in range(NG)] for f in range(16)]
        mcsT = [[T(MCS[f, :, g * G:(g + 1) * G], f"mcs{f}_{g}") for g in range(NG)] for f in range(16)]

        def unit_slice(u):
            nco = 384 if u < 16 else 512
            return lambda sap: sap[:, :, 0:nco]
        stream = self.Stream(self, [(winT[u], unit_slice(u)) for u in unit_seq], RING)
        ucount = [0]
        rot_i = [0]
        cnt = {"xt": 0, "ro": 0, "mco": 0, "vo": 0, "go": 0, "cv": 0}

        norm_state = {}

        def norm_a(gidx, tt):
            kind, tok0, T_, ms = groups[gidx]
            if tt >= T_ // 128:
                return
            i2 = cnt["xt"] % 2
            xt, ss, xn = XT[i2], SS[i2], XNs[i2]
            cnt["xt"] += 1
            norm_state[(gidx, tt)] = xn
            r0 = tok0 + tt * 128
            dma("sp", xt.ap, xall[r0:r0 + 128, :], [], [xt], xt.ds)
            dve(lambda e: e.memset(ss.ap[:, 0:1], 0.0), [], [ss])
            act(lambda e: e.activation(out=xn.ap, in_=xt.ap, func=AF.Square, accum_out=ss.ap[:, 0:1]),
                [xt], [xn, ss])
            act(lambda e: e.activation(out=ss.ap[:, 1:2], in_=ss.ap[:, 0:1], func=AF.Sqrt, scale=1.0 / D,
                                       bias=epsc), [ss, CC], [ss])
            dve(lambda e: e.reciprocal(out=ss.ap[:, 2:3], in_=ss.ap[:, 1:2]), [ss], [ss])
            act(lambda e: e.activation(out=xn.ap, in_=xt.ap, func=AF.Copy, scale=ss.ap[:, 2:3]), [xt, ss], [xn])

        def norm_b(gidx, tt):
            kind, tok0, T_, ms = groups[gidx]
            if tt >= T_ // 128:
                return
            xn = norm_state.pop((gidx, tt))
            aT = AT[gidx % 2]
            gcol, scol = (2, 3) if ms == 1 else (0, 1)
            for q4 in range(4):
                bk = self.bank()
                for j in range(4):
                    dt_ = q4 * 4 + j
                    pe(lambda e, bk=bk, j=j, dt_=dt_: e.transpose(
                        out=bk.ap[:, j * 128:(j + 1) * 128], in_=xn.ap[:, dt_ * 128:(dt_ + 1) * 128],
                        identity=ident), [xn, CT], [bk])
                for j in range(4):
                    dt_ = q4 * 4 + j
                    act(lambda e, bk=bk, j=j, dt_=dt_: e.activation(
                        out=aT.ap[:, dt_, tt * 128:(tt + 1) * 128], in_=bk.ap[:, j * 128:(j + 1) * 128],
                        func=AF.Identity, scale=GS.ap[:, gcol, dt_:dt_ + 1], bias=GS.ap[:, scol, dt_:dt_ + 1]),
                        [bk, GS], [aT])

        def do_sched(items, gidx):
            for (what, tt) in items:
                if what == "a":
                    norm_a(gidx + 1, tt)
                elif what == "b":
                    norm_b(gidx + 1, tt)
                else:
                    rope_tables(groups[gidx + 1][1], groups[gidx + 1][2])

        def fm_tile(aT, T_, slot, c0):
            bk = self.bank()
            for kc in range(16):
                mm(bk.ap[:, 0:T_], slot.ap[:, kc, c0:c0 + 128], aT.ap[:, kc, 0:T_], kc == 0, kc == 15,
                   [slot, aT], [bk])
            return bk

        def rope_tables(tok0, T_):
            dma("sp", POSB.ap[:, 0:T_], pos[tok0:tok0 + T_].partition_broadcast(128), [], [POSB], POSB.ds)
            dve(lambda e: e.tensor_scalar(out=ANG.ap[:, 0:T_], in0=POSB.ap[:, 0:T_], scalar1=invfT.ap[:, 0:1],
                                          scalar2=None, op0=ALU.mult), [POSB, invfT], [ANG])
            dve(lambda e: e.tensor_scalar(out=KF.ap[:, 0:T_], in0=ANG.ap[:, 0:T_], scalar1=1.0 / (2 * math.pi),
                                          scalar2=None, op0=ALU.mult), [ANG], [KF])
            dve(lambda e: e.tensor_copy(out=KI.ap[:, 0:T_], in_=KF.ap[:, 0:T_]), [KF], [KI])
            dve(lambda e: e.tensor_copy(out=KF.ap[:, 0:T_], in_=KI.ap[:, 0:T_]), [KI], [KF])
            dve(lambda e: e.scalar_tensor_tensor(out=RR.ap[:, 0:T_], in0=KF.ap[:, 0:T_], scalar=-TWO_PI_HI,
                                                 in1=ANG.ap[:, 0:T_], op0=ALU.mult, op1=ALU.add), [KF, ANG], [RR])
            dve(lambda e: e.scalar_tensor_tensor(out=RR.ap[:, 0:T_], in0=KF.ap[:, 0:T_], scalar=-TWO_PI_LO,
                                                 in1=RR.ap[:, 0:T_], op0=ALU.mult, op1=ALU.add), [KF, RR], [RR])
            dve(lambda e: e.tensor_scalar(out=RR.ap[:, 0:T_], in0=RR.ap[:, 0:T_], scalar1=-3.14159,
                                          scalar2=3.14159, op0=ALU.max, op1=ALU.min), [RR], [RR])
            act(lambda e: e.activation(out=SIN.ap[:, 0:T_], in_=RR.ap[:, 0:T_], func=AF.Sin), [RR], [SIN])
            act(lambda e: e.activation(out=RC.ap[:, 0:T_], in_=RR.ap[:, 0:T_], func=AF.Abs), [RR], [RC])
            act(lambda e: e.activation(out=COS.ap[:, 0:T_], in_=RC.ap[:, 0:T_], func=AF.Sin, scale=-1.0,
                                       bias=hpic), [RC, CC], [COS])

        def rope_pair(aT, T_, slot, c0, dstT_list, tile0, chunk0):
            b1 = fm_tile(aT, T_, slot, c0)
            b2 = fm_tile(aT, T_, slot, c0 + 128)
            ta, tb, tc, td = RT
            dve(lambda e: e.tensor_tensor(out=ta.ap[:, 0:T_], in0=b1.ap[:, 0:T_], in1=COS.ap[:, 0:T_], op=ALU.mult),
                [b1, COS], [ta])
            dve(lambda e: e.tensor_tensor(out=tb.ap[:, 0:T_], in0=b2.ap[:, 0:T_], in1=SIN.ap[:, 0:T_], op=ALU.mult),
                [b2, SIN], [tb])
            dve(lambda e: e.tensor_tensor(out=tc.ap[:, 0:T_], in0=b1.ap[:, 0:T_], in1=SIN.ap[:, 0:T_], op=ALU.mult),
                [b1, SIN], [tc])
            dve(lambda e: e.tensor_tensor(out=td.ap[:, 0:T_], in0=b2.ap[:, 0:T_], in1=COS.ap[:, 0:T_], op=ALU.mult),
                [b2, COS], [td])
            r1 = RO[cnt["ro"] % 4]
            r2 = RO[(cnt["ro"] + 1) % 4]
            cnt["ro"] += 2
            dve(lambda e: e.tensor_tensor(out=r1.ap[:, 0:T_], in0=ta.ap[:, 0:T_], in1=tb.ap[:, 0:T_], op=ALU.subtract),
                [ta, tb], [r1])
            dve(lambda e: e.tensor_tensor(out=r2.ap[:, 0:T_], in0=tc.ap[:, 0:T_], in1=td.ap[:, 0:T_], op=ALU.add),
                [tc, td], [r2])
            nchk = T_ // 128
            for r, tl in ((r1, tile0), (r2, tile0 + 1)):
                for cc in range(nchk):
                    dT = dstT_list[chunk0 + cc]
                    dma("sp", dT.ap[:, tl, :], r.ap[:, cc * 128:(cc + 1) * 128], [r], [dT], r.ds, append=True)

        gi_own = 0
        for tt in range(4):
            norm_a(0, tt)
            norm_b(0, tt)
        rope_tables(groups[0][1], groups[0][2])
        for gidx, (kind, tok0, T_, ms) in enumerate(groups):
            aT = AT[gidx % 2]
            chunk0 = tok0 // 128
            units = kv_units if kind == "kv" else full_units
            nu = len(units)
            sched = {}
            if gidx + 1 < len(groups):
                last_rope = 3 if kind == "kv" else 31
                sched.setdefault(last_rope, []).append(("rope", 0))
                for tt in range(4):
                    pa = (last_rope + 1 + 4 * tt) if kind == "full" else (4 + 2 * tt)
                    pb = pa + (2 if kind == "full" else 1)
                    sched.setdefault(min(pa, nu - 1), []).append(("a", tt))
                    sched.setdefault(min(pb, nu - 1), []).append(("b", tt))
            for ui, u in enumerate(units):
                if ui > 0:
                    do_sched(sched.get(ui - 1, []), gidx)
                slot = stream.get(ucount[0])
                ucount[0] += 1
                if u < 16:
                    ct = u
                    bc_ = fm_tile(aT, T_, slot, 0)
                    bx_ = fm_tile(aT, T_, slot, 128)
                    bb_ = fm_tile(aT, T_, slot, 256)
                    i2 = cnt["cv"] % 2
                    cnt["cv"] += 1
                    csb, up, yc = CSB[i2], UP[i2], YC[i2]
                    act(lambda e, bc_=bc_, csb=csb: e.activation(out=csb.ap, in_=bc_.ap, func=AF.Copy), [bc_], [csb])
                    dve(lambda e, up=up, csb=csb, bx_=bx_: e.tensor_tensor(
                        out=up.ap[:, :, 1:65], in0=csb.ap.rearrange("p (r w) -> p r w", w=64),
                        in1=bx_.ap.rearrange("p (r w) -> p r w", w=64), op=ALU.mult), [csb, bx_], [up])
                    ycv = yc.ap.rearrange("p (r w) -> p r w", w=64)
                    dve(lambda e, up=up, ycv=ycv, ct=ct: e.tensor_scalar(
                        out=ycv, in0=up.ap[:, :, 1:65], scalar1=VC.ap[:, 64 + 16 + ct:64 + 16 + ct + 1],
                        scalar2=None, op0=ALU.mult), [up, VC], [yc])
                    dve(lambda e, up=up, ycv=ycv, ct=ct: e.scalar_tensor_tensor(
                        out=ycv, in0=up.ap[:, :, 0:64], scalar=VC.ap[:, 64 + ct:64 + ct + 1], in1=ycv,
                        op0=ALU.mult, op1=ALU.add), [up, VC, yc], [yc])
                    dve(lambda e, up=up, ycv=ycv, ct=ct: e.scalar_tensor_tensor(
                        out=ycv, in0=up.ap[:, :, 2:66], scalar=VC.ap[:, 64 + 32 + ct:64 + 32 + ct + 1], in1=ycv,
                        op0=ALU.mult, op1=ALU.add), [up, VC, yc], [yc])
                    dve(lambda e, yc=yc, bb_=bb_, ct=ct: e.tensor_tensor(
                        out=YBC.ap[:, ct, :], in0=yc.ap, in1=bb_.ap, op=ALU.mult), [yc, bb_], [YBC])
                elif u < 20:
                    for j in range(4):
                        bk = fm_tile(aT, T_, slot, j * 128)
                        act(lambda e, bk=bk, j=j: e.activation(out=SG.ap[:, j, :], in_=bk.ap, func=AF.Sigmoid),
                            [bk], [SG])
                elif u < 24:
                    fq = u - 20
                    for j in range(4):
                        ft = fq * 4 + j
                        bk = self.bank()
                        for kc in range(16):
                            mm(bk.ap, slot.ap[:, kc, j * 128:(j + 1) * 128], YBC.ap[:, kc, :], kc == 0, kc == 15,
                               [slot, YBC], [bk])
                        mo = MCO[cnt["mco"] % 2]
                        cnt["mco"] += 1
                        dve(lambda e, bk=bk, j=j, mo=mo: e.tensor_tensor(out=mo.ap, in0=bk.ap, in1=SG.ap[:, j, :],
                                                                          op=ALU.mult), [bk, SG], [mo])
                        dT = mcsT[ft][gi_own]
                        dma("sp", dT.ap, mo.ap, [mo], [dT], mo.ds)
                elif u < 28:
                    b_ = u - 24
                    for hh in range(2):
                        rope_pair(aT, T_, slot, hh * 256, qtT, (b_ * 2 + hh) * 2, chunk0)
                elif u < 32:
                    b_ = u - 28
                    for hh in range(2):
                        rope_pair(aT, T_, slot, hh * 256, ktT, (b_ * 2 + hh) * 2, chunk0)
                elif u < 40:
                    vb = u - 32
                    for tt in range(T_ // 128):
                        bk = self.bank()
                        for kc in range(16):
                            mm(bk.ap, aT.ap[:, kc, tt * 128:(tt + 1) * 128], slot.ap[:, kc, :], kc == 0, kc == 15,
                               [slot, aT], [bk])
                        vo = VO[cnt["vo"] % 3]
                        cnt["vo"] += 1
                        act(lambda e, bk=bk, vo=vo: e.activation(out=vo.ap, in_=bk.ap, func=AF.Copy), [bk], [vo])
                        dT = vsT[chunk0 + tt]
                        dma("sp", dT.ap[:, vb * 512:(vb + 1) * 512], vo.ap, [vo], [dT], vo.ds, append=True)
                elif u < 48:
                    gb = u - 40
                    for j in range(4):
                        gt_ = gb * 4 + j
                        bk = fm_tile(aT, T_, slot, j * 128)
                        go = GO[cnt["go"] % 3]
                        cnt["go"] += 1
                        act(lambda e, bk=bk, go=go: e.activation(out=go.ap, in_=bk.ap, func=AF.Silu), [bk], [go])
                        for cc in range(T_ // 128):
                            dT = gtT[chunk0 + cc]
                            dma("sp", dT.ap[:, gt_, :], go.ap[:, cc * 128:(cc + 1) * 128], [go], [dT], go.ds,
                                append=True)
                else:
                    fq = u - 48
                    for j in range(4):
                        ft = fq * 4 + j
                        bk = fm_tile(aT, T_, slot, j * 128)
                        go = GO[cnt["go"] % 3]
                        cnt["go"] += 1
                        act(lambda e, bk=bk, go=go: e.activation(out=go.ap, in_=bk.ap, func=AF.Sigmoid), [bk], [go])
                        dT = sgrT[ft][gi_own]
                        dma("sp", dT.ap, go.ap, [go], [dT], go.ds)
            do_sched(sched.get(nu - 1, []), gidx)
            if kind == "full":
                gi_own += 1

        self.phase_begin()
        ktT_, vsT_ = ktT, vsT
        sbsT = [T(SBS[c], f"sbs{c}") for c in range(NC)]
        gonT = [T(GON[c], f"gon{c}") for c in range(NC)]
        cntS = {"kv": 0, "ktl": 0, "sbf": 0}
        pool = lambda fn, reads, writes: self.R.op("pool", fn, [r.buf for r in reads], [w.buf for w in writes])

        def sweep_bufs(nk=2):
            S32 = [self.alloc([512], F32, f"s32_{i}") for i in range(16)]
            KTs = [self.alloc([16, 128], BF16, f"kts{i}", ds=True) for i in range(nk)]
            VTs = [self.alloc([NH * DV], BF16, f"vts{i}", ds=True) for i in range(2)]
            KTL = [self.alloc([8, 256], BF16, f"ktl{i}") for i in range(2)]
            return S32, KTs, VTs, KTL

        def load_kv(c):
            i = cntS["kv"]
            cntS["kv"] += 1
            kts, vts = KTs[i % len(KTs)], VTs[i % 2]
            dma("sp", kts.ap, ktT[c].ap, [ktT[c]], [kts], kts.ds)
            dma("sp", vts.ap, vsT[c].ap, [vsT[c]], [vts], vts.ds)
            return kts, vts

        def k_tilde(kts, di):
            ktl = KTL[cntS["ktl"] % 2]
            cntS["ktl"] += 1
            bks = [self.bank(), self.bank()]
            for j in range(16):
                bkv = bks[j // 8].ap.bitcast(BF16)
                pe(lambda e, bkv=bkv, j=j: e.transpose(out=bkv[:, (j % 8) * 128:(j % 8 + 1) * 128],
                                                       in_=kts.ap[:, j, :], identity=identb.ap),
                   [kts, identb], [bks[j // 8]])
            for h in range(8):
                bkv = bks[h // 4].ap.bitcast(BF16)
                act(lambda e, bkv=bkv, h=h: e.activation(out=ktl.ap[:, h, :],
                                                         in_=bkv[:, (h % 4) * 256:(h % 4 + 1) * 256],
                                                         func=AF.Copy, scale=KDEC.ap[:, di, h:h + 1]),
                    [bks[h // 4], KDEC], [ktl])
            return ktl

        def state_update(ktl, vts, di, sbf):
            for h in range(8):
                for dkt in range(2):
                    bk = self.bank()
                    mm(bk.ap, ktl.ap[:, h, dkt * 128:(dkt + 1) * 128], vts.ap[:, h * 512:(h + 1) * 512], True, True,
                       [ktl, vts], [bk])
                    s_ = S32[h * 2 + dkt]
                    dve(lambda e, s_=s_, bk=bk, h=h: e.scalar_tensor_tensor(
                        out=s_.ap, in0=s_.ap, scalar=CD.ap[:, di, h:h + 1], in1=bk.ap, op0=ALU.mult, op1=ALU.add),
                        [s_, CD, bk], [s_])
                    if sbf is not None:
                        if (h * 2 + dkt) % 4 == 3:
                            pool(lambda e, s_=s_, h=h, dkt=dkt: e.tensor_copy(out=sbf.ap[:, h * 2 + dkt, :],
                                                                              in_=s_.ap), [s_, sbf], [sbf])
                        else:
                            act(lambda e, s_=s_, h=h, dkt=dkt: e.activation(out=sbf.ap[:, h * 2 + dkt, :],
                                                                            in_=s_.ap, func=AF.Copy),
                                [s_, sbf], [sbf])

        def zero_state():
            for s_ in S32:
                dve(lambda e, s_=s_: e.memset(s_.ap, 0.0), [], [s_])

        S32, KTs, VTs, KTL = sweep_bufs()
        SBF_all = [self.alloc([16, 512], BF16, f"sbf{i}", ds=True) for i in range(2)]
        zero_state()
        mod_setup_bufs()
        late_slots = [wm_load(*late[0])]
        seqB = [2 * NC + 1, 2 * NC] + list(range(2 * NC - 1, NC - 1, -1)) + list(range(NC - 1, 0, -1))
        per_it = -(-len(late) // len(seqB))
        li = 0
        for idx, c in enumerate(seqB):
            for _ in range(per_it):
                if li < len(late):
                    if li + 1 < len(late):
                        late_slots.append(wm_load(*late[li + 1]))
                    mod_piece(late[li][0], late[li][1], late_slots[li])
                    li += 1
                    if li == len(late):
                        gs_make(4, 5, 2, 3, 0, 48)
            kts, vts = load_kv(c)
            ktl = k_tilde(kts, 1)
            snap = 1 <= c <= NC
            sbf = None
            if snap:
                sbf = SBF_all[cntS["sbf"] % 2]
                cntS["sbf"] += 1
            state_update(ktl, vts, 1, sbf)
            if snap:
                dma("sp", SBS[c - 1], sbf.ap, [sbf], [sbsT[c - 1]], sbf.ds)

        self.phase_begin()
        QD = [self.alloc([16, 128], F32, f"qd{i}") for i in range(2)]
        MK = self.alloc([8, 128], F32, "mask")
        tmpm = self.alloc([128], F32, "tmpm")
        for h in range(8):
            for di, src in ((0, fi1), (1, fi2)):
                for j in range(2):
                    act(lambda e, h=h, di=di, src=src, j=j: e.activation(
                        out=QD[di].ap[:, 2 * h + j, :], in_=src, func=AF.Exp, scale=LG.ap[:, di, h:h + 1]),
                        [CT, LG, QD[di]], [QD[di]])
            act(lambda e, h=h: e.activation(out=MK.ap[:, h, :], in_=relp, func=AF.Exp, scale=LG.ap[:, 0, h:h + 1]),
                [CT, LG, MK], [MK])
            dve(lambda e, h=h: e.tensor_tensor(out=MK.ap[:, h, :], in0=MK.ap[:, h, :], in1=mge, op=ALU.mult),
                [MK, CT], [MK])
            act(lambda e, h=h: e.activation(out=tmpm.ap, in_=reln, func=AF.Exp, scale=LG.ap[:, 1, h:h + 1]),
                [CT, LG], [tmpm])
            dve(lambda e: e.tensor_tensor(out=tmpm.ap, in0=tmpm.ap, in1=mle, op=ALU.mult), [tmpm, CT], [tmpm])
            dve(lambda e, h=h: e.tensor_tensor(out=MK.ap[:, h, :], in0=MK.ap[:, h, :], in1=tmpm.ap, op=ALU.add),
                [MK, tmpm], [MK])
        dve(lambda e: e.tensor_scalar(out=MK.ap, in0=MK.ap, scalar1=1.0 / 16.0, scalar2=None, op0=ALU.mult),
            [MK], [MK])
        S32, KTs, VTs, KTL = sweep_bufs(1)
        sbf_f = self.alloc([16, 512], BF16, "sbf_f")
        zero_state()
        QTs = [self.alloc([16, 128], BF16, f"qts{i}", ds=True) for i in range(1)]
        GTs = [self.alloc([32, 128], BF16, f"gts{i}", ds=True) for i in range(1)]
        SBcs = [self.alloc([16, 512], BF16, f"sbc{i}", ds=True) for i in range(2)]
        QFs = [self.alloc([16, 128], BF16, f"qf{i}") for i in range(2)]
        QBs = [self.alloc([16, 128], BF16, f"qb{i}") for i in range(2)]
        STs = [self.alloc([8, 128], BF16, f"st{i}") for i in range(2)]
        ONs = [self.alloc([512], BF16, f"on{i}") for i in range(8)]
        GONs = [self.alloc([32, 128], BF16, f"gon{i}", ds=True) for i in range(2)]
        STAT = [self.alloc([64], F32, f"stat{i}") for i in range(2)]
        OSB = [self.alloc([512], F32, f"osb{i}") for i in range(8)]
        for ci, c in enumerate([2 * NC, 2 * NC + 1]):
            kts, vts = load_kv(c)
            ktl = k_tilde(kts, 0)
            state_update(ktl, vts, 0, sbf_f if ci == 1 else None)
        env = {}

        def f_loads(c):
            kts, vts = load_kv(c)
            qts = QTs[0]
            SBc = SBcs[c % 2]
            dma("sp", qts.ap, qtT[c].ap, [qtT[c]], [qts], qts.ds)
            dma("sp", SBc.ap, SBS[c], [sbsT[c]], [SBc], SBc.ds)
            env[c] = dict(kts=kts, vts=vts, qts=qts, SBc=SBc)

        def f_A(c):
            e_ = env[c]
            kts, qts = e_["kts"], e_["qts"]
            QF, QB = QFs[c % 2], QBs[c % 2]
            pool(lambda e: e.tensor_tensor(out=QF.ap, in0=qts.ap, in1=QD[0].ap, op=ALU.mult), [qts, QD[0]], [QF])
            pool(lambda e: e.tensor_tensor(out=QB.ap, in0=qts.ap, in1=QD[1].ap, op=ALU.mult), [qts, QD[1]], [QB])
            stt = STs[c % 2]
            for hb in range(2):
                bk = self.bank()
                for hh in range(4):
                    h = hb * 4 + hh
                    for dkt in range(2):
                        mm(bk.ap[:, hh * 128:(hh + 1) * 128], kts.ap[:, 2 * h + dkt, :], qts.ap[:, 2 * h + dkt, :],
                           dkt == 0, dkt == 1, [kts, qts], [bk])
                dve(lambda e, bk=bk, hb=hb: e.tensor_tensor(
                    out=stt.ap[:, hb * 4:(hb + 1) * 4, :], in0=bk.ap.rearrange("p (h i) -> p h i", i=128),
                    in1=MK.ap[:, hb * 4:(hb + 1) * 4, :], op=ALU.mult), [bk, MK, stt], [stt])
            e_["ktl"] = k_tilde(kts, 0)
            e_.update(QF=QF, QB=QB, stt=stt)
            if c + 1 < NC:
                f_loads(c + 1)

        def f_B(c):
            e_ = env[c]
            stt, vts, QF, QB, SBc, ktl = e_["stt"], e_["vts"], e_["QF"], e_["QB"], e_["SBc"], e_["ktl"]
            sa = STAT[c % 2]
            dve(lambda e: e.memset(sa.ap[:, 0:16], 0.0), [], [sa])
            for h in range(8):
                bk = self.bank()
                mm(bk.ap, stt.ap[:, h, :], vts.ap[:, h * 512:(h + 1) * 512], True, False, [stt, vts], [bk])
                for dkt in range(2):
                    mm(bk.ap, QF.ap[:, 2 * h + dkt, :], sbf_f.ap[:, 2 * h + dkt, :], False, False, [QF, sbf_f], [bk])
                for dkt in range(2):
                    mm(bk.ap, QB.ap[:, 2 * h + dkt, :], SBc.ap[:, 2 * h + dkt, :], False, dkt == 1, [QB, SBc], [bk])
                ob, on = OSB[h], ONs[h]
                act(lambda e, bk=bk, h=h, ob=ob: e.activation(
                    out=ob.ap, in_=bk.ap, func=AF.Identity, accum_out=sa.ap[:, h:h + 1]), [bk, sa], [ob, sa])
                act(lambda e, h=h, ob=ob, on=on: e.activation(
                    out=on.ap, in_=ob.ap, func=AF.Square, accum_out=sa.ap[:, 8 + h:9 + h]), [ob, sa], [on, sa])
            if c < NC - 1:
                state_update(ktl, vts, 0, sbf_f)

        def f_C(c):
            sa = STAT[c % 2]
            dve(lambda e: e.tensor_scalar(out=sa.ap[:, 16:24], in0=sa.ap[:, 0:8], scalar1=1.0 / DV,
                                          scalar2=None, op0=ALU.mult), [sa], [sa])
            dve(lambda e: e.tensor_tensor(out=sa.ap[:, 24:32], in0=sa.ap[:, 16:24], in1=sa.ap[:, 16:24],
                                          op=ALU.mult), [sa], [sa])
            dve(lambda e: e.scalar_tensor_tensor(out=sa.ap[:, 32:40], in0=sa.ap[:, 8:16], scalar=1.0 / DV,
                                                 in1=sa.ap[:, 24:32], op0=ALU.mult, op1=ALU.subtract), [sa], [sa])
            act(lambda e: e.activation(out=sa.ap[:, 40:48], in_=sa.ap[:, 32:40], func=AF.Sqrt, bias=epsc),
                [sa, CC], [sa])
            dve(lambda e: e.reciprocal(out=sa.ap[:, 48:56], in_=sa.ap[:, 40:48]), [sa], [sa])
            dve(lambda e: e.scalar_tensor_tensor(out=sa.ap[:, 56:64], in0=sa.ap[:, 16:24], scalar=-1.0,
                                                 in1=sa.ap[:, 48:56], op0=ALU.mult, op1=ALU.mult), [sa], [sa])
            for h in range(8):
                ob, on = OSB[h], ONs[h]
                act(lambda e, ob=ob, on=on, h=h: e.activation(
                    out=on.ap, in_=ob.ap, func=AF.Identity, scale=sa.ap[:, 48 + h:49 + h],
                    bias=sa.ap[:, 56 + h:57 + h]), [ob, sa], [on])

        def f_D(c):
            gts = GTs[0]
            gon = GONs[c % 2]
            dma("sp", gts.ap, gtT[c].ap, [gtT[c]], [gts], gts.ds)
            for h in range(8):
                on = ONs[h]
                bt = self.bank()
                btv = bt.ap.bitcast(BF16)
                for j in range(4):
                    pe(lambda e, btv=btv, j=j, on=on: e.transpose(out=btv[:, j * 128:(j + 1) * 128],
                                                                  in_=on.ap[:, j * 128:(j + 1) * 128],
                                                                  identity=identb.ap), [on, identb], [bt])
                dve(lambda e, btv=btv, h=h: e.tensor_tensor(
                    out=gon.ap[:, h * 4:(h + 1) * 4, :], in0=btv[:, 0:512].rearrange("p (a b) -> p a b", b=128),
                    in1=gts.ap[:, h * 4:(h + 1) * 4, :], op=ALU.mult), [bt, gts, gon], [gon])
            dma("sp", GON[c], gon.ap, [gon], [gonT[c]], gon.ds)
            del env[c]

        f_loads(0)
        f_A(0)
        f_B(0)
        f_C(0)
        for c in range(NC):
            if c + 1 < NC:
                f_A(c + 1)
            f_D(c)
            if c + 1 < NC:
                f_B(c + 1)
                f_C(c + 1)

        self.phase_begin()
        M2 = self.alloc([D], F32, "m2bc", ds=True)
        dma("sp", M2.ap, BCS[0], [BCST[0]], [M2], M2.ds)
        GNg = self.alloc([32, G], BF16, "gong", ds=True)
        SGs = [self.alloc([G], BF16, f"sgs{i}", ds=True) for i in range(2)]
        MCs = [self.alloc([G], F32, f"mcs{i}", ds=True) for i in range(2)]
        TMP = [self.alloc([G], F32, f"tmp{i}") for i in range(2)]
        MT = self.alloc([16, G], BF16, "mT")
        RRO = [self.alloc([32, 128], BF16, f"rro{i}", ds=True) for i in range(3)]
        RWO = [self.alloc([16, 512], BF16, f"rwo{i}", ds=True) for i in range(2)]
        XTs = [self.alloc([D], F32, f"x1t{i}", ds=True) for i in range(4)]
        XN2s = [self.alloc([D], F32, f"xn2_{i}") for i in range(2)]
        SS2 = [self.alloc([4], F32, f"ss2_{i}") for i in range(2)]
        A2 = self.alloc([16, G], BF16, "a2", ds=True)
        x1T = [T(X1[t * 128:(t + 1) * 128, :], f"x1_{t}") for t in range(NC)]
        a2T = [T(A2T[g], f"a2t{g}") for g in range(NG)]
        ro_units = []
        wo_units = []
        for g in range(NG):
            ro_units += [(wroT[f], lambda sap: sap) for f in range(16)]
            wo_units += [(woT[c], lambda sap: sap) for c in range(4)]
        st_ro = self.Stream(self, ro_units, RRO)
        st_wo = self.Stream(self, wo_units, RWO)

        def p3_load(g):
            for cc in range(4):
                c = g * 4 + cc
                dma("sp", GNg.ap[:, :, cc * 128:(cc + 1) * 128], GON[c], [gonT[c]], [GNg], GNg.ds,
                    append=(cc > 0))

        def p3_ret(g, half):
            for ft in range(half * 8, half * 8 + 8):
                slot = st_ro.get(g * 16 + ft)
                sgs, mcs, tmp = SGs[ft % 2], MCs[ft % 2], TMP[ft % 2]
                dma("sp", sgs.ap, sgrT[ft][g].ap, [sgrT[ft][g]], [sgs], sgs.ds)
                dma("sp", mcs.ap, mcsT[ft][g].ap, [mcsT[ft][g]], [mcs], mcs.ds)
                bk = self.bank()
                for kc in range(32):
                    mm(bk.ap, slot.ap[:, kc, :], GNg.ap[:, kc, :], kc == 0, kc == 31, [slot, GNg], [bk])
                dve(lambda e, bk=bk, sgs=sgs, tmp=tmp: e.tensor_tensor(out=tmp.ap, in0=bk.ap, in1=sgs.ap, op=ALU.mult),
                    [bk, sgs], [tmp])
                dve(lambda e, tmp=tmp, mcs=mcs, ft=ft: e.tensor_tensor(out=MT.ap[:, ft, :], in0=tmp.ap, in1=mcs.ap,
                                                                       op=ALU.add), [tmp, mcs, MT], [MT])

        def p3_wo(g):
            for tt in range(4):
                xt = XTs[tt]
                r0 = g * G + tt * 128
                dma("sp", xt.ap, xall[r0:r0 + 128, :], [], [xt], xt.ds)
            for cb in range(4):
                slot = st_wo.get(g * 4 + cb)
                for tt in range(4):
                    bk = self.bank()
                    for kc in range(16):
                        mm(bk.ap, MT.ap[:, kc, tt * 128:(tt + 1) * 128], slot.ap[:, kc, :], kc == 0, kc == 15,
                           [slot, MT], [bk])
                    xt = XTs[tt]
                    tmp = TMP[tt % 2]
                    dve(lambda e, bk=bk, tmp=tmp, cb=cb: e.tensor_tensor(
                        out=tmp.ap, in0=bk.ap, in1=M2.ap[:, cb * 512:(cb + 1) * 512], op=ALU.mult), [bk, M2], [tmp])
                    dve(lambda e, xt=xt, tmp=tmp, cb=cb: e.tensor_tensor(
                        out=xt.ap[:, cb * 512:(cb + 1) * 512], in0=xt.ap[:, cb * 512:(cb + 1) * 512], in1=tmp.ap,
                        op=ALU.add), [xt, tmp], [xt])

        def p3_na(g, tt):
            xt = XTs[tt]
            ss = SS2[tt % 2]
            xn = XN2s[tt % 2]
            c = g * 4 + tt
            dma("sp", x1T[c].ap, xt.ap, [xt], [x1T[c]], xt.ds)
            dve(lambda e: e.memset(ss.ap[:, 0:1], 0.0), [], [ss])
            act(lambda e: e.activation(out=xn.ap, in_=xt.ap, func=AF.Square, accum_out=ss.ap[:, 0:1]), [xt], [xn, ss])
            act(lambda e: e.activation(out=ss.ap[:, 1:2], in_=ss.ap[:, 0:1], func=AF.Sqrt, scale=1.0 / D,
                                       bias=epsc), [ss, CC], [ss])
            dve(lambda e: e.reciprocal(out=ss.ap[:, 2:3], in_=ss.ap[:, 1:2]), [ss], [ss])
            act(lambda e: e.activation(out=xn.ap, in_=xt.ap, func=AF.Copy, scale=ss.ap[:, 2:3]), [xt, ss], [xn])

        def p3_nb(g, tt):
            xn = XN2s[tt % 2]
            for q4 in range(4):
                bk = self.bank()
                for j in range(4):
                    dt_ = q4 * 4 + j
                    pe(lambda e, bk=bk, j=j, dt_=dt_: e.transpose(
                        out=bk.ap[:, j * 128:(j + 1) * 128], in_=xn.ap[:, dt_ * 128:(dt_ + 1) * 128],
                        identity=ident), [xn, CT], [bk])
                for j in range(4):
                    dt_ = q4 * 4 + j
                    act(lambda e, bk=bk, j=j, dt_=dt_: e.activation(
                        out=A2.ap[:, dt_, tt * 128:(tt + 1) * 128], in_=bk.ap[:, j * 128:(j + 1) * 128],
                        func=AF.Identity, scale=GS.ap[:, 4, dt_:dt_ + 1], bias=GS.ap[:, 5, dt_:dt_ + 1]),
                        [bk, GS, A2], [A2])
            if tt == 3:
                dma("sp", A2T[g], A2.ap, [A2], [a2T[g]], A2.ds)

        for g in range(NG + 1):
            if g < NG:
                p3_load(g)
                p3_ret(g, 0)
            if g > 0:
                p3_nb(g - 1, 0)
                p3_nb(g - 1, 1)
                p3_na(g - 1, 2)
                p3_na(g - 1, 3)
            if g < NG:
                p3_ret(g, 1)
            if g > 0:
                p3_nb(g - 1, 2)
                p3_nb(g - 1, 3)
            if g < NG:
                p3_wo(g)
                p3_na(g, 0)
                p3_na(g, 1)

        self.phase_begin()
        M5 = self.alloc([D], F32, "m5bc", ds=True)
        FGB = self.alloc([D], F32, "fgbc", ds=True)
        dma("sp", M5.ap, BCS[1], [BCST[1]], [M5], M5.ds)
        dma("sp", FGB.ap, fg.partition_broadcast(128), [], [FGB], FGB.ds)
        A2s = self.alloc([16, G], BF16, "a2s", ds=True)
        HT = self.alloc([64, G], BF16, "hT")
        X2 = [self.alloc([D], F32, f"x2_{i}", ds=True) for i in range(4)]
        X1s = [self.alloc([512], F32, f"x1s{i}", ds=True) for i in range(2)]
        R1 = [self.alloc([16, 256], BF16, f"rf1_{i}", ds=True) for i in range(3)]
        R2 = [self.alloc([8, 512], BF16, f"rf2_{i}", ds=True) for i in range(3)]
        RL = [self.alloc([G], F32, f"rl{i}") for i in range(2)]
        SS3 = [self.alloc([4], F32, f"ss3_{i}") for i in range(2)]
        JK = self.alloc([D], F32, "jk")
        f1_units = []
        f2_units = []
        for g in range(NG):
            f1_units += [(wf1T[u], lambda sap: sap) for u in range(32)]
            for c in range(4):
                f2_units += [(wf2T[c][s], lambda sap: sap) for s in range(8)]
        st_f1 = self.Stream(self, f1_units, R1)
        st_f2 = self.Stream(self, f2_units, R2)
        outT = [T(out[t * 128:(t + 1) * 128, :], f"out{t}") for t in range(NC)]
        c1 = 0
        for g in range(NG):
            dma("sp", A2s.ap, A2T[g], [a2T[g]], [A2s], A2s.ds)
            for u in range(32):
                slot = st_f1.get(g * 32 + u)
                for j in range(2):
                    ht = u * 2 + j
                    bk = self.bank()
                    for kc in range(16):
                        mm(bk.ap, slot.ap[:, kc, j * 128:(j + 1) * 128], A2s.ap[:, kc, :], kc == 0, kc == 15,
                           [slot, A2s], [bk])
                    rl = RL[ht % 2]
                    act(lambda e, bk=bk, rl=rl: e.activation(out=rl.ap, in_=bk.ap, func=AF.Relu), [bk], [rl])
                    dve(lambda e, rl=rl, ht=ht: e.tensor_tensor(out=HT.ap[:, ht, :], in0=rl.ap, in1=rl.ap, op=ALU.mult),
                        [rl, HT], [HT])
            for cb in range(4):
                bks = [self.bank() for _ in range(4)]
                for s in range(8):
                    slot = st_f2.get((g * 4 + cb) * 8 + s)
                    for tt in range(4):
                        for kc in range(8):
                            mm(bks[tt].ap, HT.ap[:, s * 8 + kc, tt * 128:(tt + 1) * 128], slot.ap[:, kc, :],
                               s == 0 and kc == 0, s == 7 and kc == 7, [slot, HT], [bks[tt]])
                for tt in range(4):
                    c = g * 4 + tt
                    x1s = X1s[c1 % 2]
                    c1 += 1
                    dma("sp", x1s.ap, X1[c * 128:(c + 1) * 128, cb * 512:(cb + 1) * 512], [x1T[c]], [x1s], x1s.ds)
                    x2 = X2[tt]
                    dve(lambda e, bk=bks[tt], x2=x2, cb=cb: e.tensor_tensor(
                        out=x2.ap[:, cb * 512:(cb + 1) * 512], in0=bk.ap, in1=M5.ap[:, cb * 512:(cb + 1) * 512],
                        op=ALU.mult), [bks[tt], M5, x2], [x2])
                    dve(lambda e, x2=x2, x1s=x1s, cb=cb: e.tensor_tensor(
                        out=x2.ap[:, cb * 512:(cb + 1) * 512], in0=x2.ap[:, cb * 512:(cb + 1) * 512], in1=x1s.ap,
                        op=ALU.add), [x2, x1s], [x2])
            for tt in range(4):
                c = g * 4 + tt
                x2 = X2[tt]
                ss = SS3[tt % 2]
                dve(lambda e, ss=ss: e.memset(ss.ap[:, 0:1], 0.0), [], [ss])
                act(lambda e, x2=x2, ss=ss: e.activation(out=JK.ap, in_=x2.ap, func=AF.Square,
                                                         accum_out=ss.ap[:, 0:1]), [x2], [JK, ss])
                act(lambda e, ss=ss: e.activation(out=ss.ap[:, 1:2], in_=ss.ap[:, 0:1], func=AF.Sqrt,
                                                  scale=1.0 / D, bias=epsc), [ss, CC], [ss])
                dve(lambda e, ss=ss: e.reciprocal(out=ss.ap[:, 2:3], in_=ss.ap[:, 1:2]), [ss], [ss])
                dve(lambda e, x2=x2, ss=ss: e.scalar_tensor_tensor(out=x2.ap, in0=x2.ap, scalar=ss.ap[:, 2:3],
                                                                   in1=FGB.ap, op0=ALU.mult, op1=ALU.mult),
                    [x2, ss, FGB], [x2])
                dma("sp", out[c * 128:(c + 1) * 128, :], x2.ap, [x2], [outT[c]], x2.ds)
        self.R.barrier()
        self.R.emit()
        return nc


def _consts():
    p = np.arange(128, dtype=np.float32)[:, None]
    i = np.arange(128, dtype=np.float32)[None, :]
    rel = i - p
    ident = (rel == 0).astype(np.float32)
    relp = np.maximum(rel, 0.0)
    reln = np.maximum(-rel, 0.0)
    mge = (rel >= 0).astype(np.float32)
    mle = (rel <= 0).astype(np.float32)
    fi1 = np.broadcast_to(i + 1.0, (128, 128))
    fi2 = np.broadcast_to(128.0 - i, (128, 128))
    pF = 127.0 - p
    pB = p
    cst = np.concatenate([ident, relp, reln, mge, mle, fi1, fi2, pF, pB], axis=1).astype(np.float32)
    half = 128
    invf = (1.0 / (np.float32(10000.0) ** np.linspace(0.0, 1.0, half, dtype=np.float32))).astype(np.float32)
    return np.ascontiguousarray(cst), invf


def make_in_maps(n_own, x, c, ctx, c_ctx, w_mod, b_mod, norm1_g, w_in, conv_w, w_conv_out,
                 ret_decay_fwd, ret_decay_bwd, w_ret_out, w_o, norm2_g, w_ff1, w_ff2, final_g):
    f = lambda a: np.ascontiguousarray(np.asarray(a, dtype=np.float32))
    x, c, ctx, c_ctx = f(x), f(c), f(ctx), f(c_ctx)
    Bn, S, _ = x.shape
    assert S == 2 * n_own
    cst, invf = _consts()
    shared = dict(cctx=f(c_ctx), w_mod=f(w_mod[0]), b_mod=f(b_mod[0]), g1=f(norm1_g[0]), g2=f(norm2_g[0]),
                  fg=f(final_g), w_in=f(w_in[0]), w_co=f(w_conv_out[0]), w_ro=f(w_ret_out[0]), w_o=f(w_o[0]),
                  w_f1=f(w_ff1[0]), w_f2=f(w_ff2[0]), cst=cst, invf=invf)
    cw = f(conv_w[0])
    dF, dB = f(ret_decay_fwd[0]), f(ret_decay_bwd[0])
    maps = []
    for b in range(Bn):
        for half in range(2):
            if half == 0:
                xl = x[b]
                cl = ctx[b]
                posl = CTX + np.arange(S, dtype=np.float32)
                posc = np.arange(CTX, dtype=np.float32)
                m = dict(decF=dF, decB=dB, conv_w=cw)
            else:
                xl = x[b, ::-1]
                cl = ctx[b, ::-1]
                posl = CTX + np.arange(S, dtype=np.float32)[::-1]
                posc = np.arange(CTX, dtype=np.float32)[::-1]
                m = dict(decF=dB, decB=dF, conv_w=np.ascontiguousarray(cw[::-1]))
            m["xall"] = np.ascontiguousarray(np.concatenate([xl, cl], axis=0))
            m["pos"] = np.ascontiguousarray(np.concatenate([posl, posc]).astype(np.float32))
            m["cvec"] = np.ascontiguousarray(c[b])
            m.update(shared)
            maps.append(m)
    return maps


_NC_CACHE = {}


def run(n_own, inputs, debug=False, trace=False):
    key = (n_own, debug)
    if key not in _NC_CACHE:
        _NC_CACHE[key] = Builder(n_own, debug=debug).build()
    nc = _NC_CACHE[key]
    maps = make_in_maps(n_own, **inputs)
    res = run_bass_kernel_spmd(nc, maps, core_ids=list(range(len(maps))), trace=trace)
    x = np.asarray(inputs["x"])
    Bn, S, _ = x.shape
    outp = np.empty((Bn, S, D), dtype=np.float32)
    for b in range(Bn):
        for half in range(2):
            o = res.results[b * 2 + half]["out"]
            if half == 0:
                outp[b, :n_own] = o
            else:
                outp[b, n_own:] = o[::-1]
    return outp, res


def kernel(**inputs):
    outp, _ = run(4096, inputs)
    return outp
```
